# Optimizing a Trainium2 kernel written in Bass

```python
import math
import jax, jax.numpy as jnp
from jax import lax
import numpy as np

D_MODEL = 2048
BATCH = 8
SEQ = 4096
DEPTH = 4

N_META = 16
EPS = 1e-6
D_MIX = D_MODEL
GLA_WIDTH = D_MIX // 4
GLA_HEADS = 4
GLA_DV = GLA_WIDTH // GLA_HEADS
GLA_DK = GLA_DV // 2
GLA_RANK = 16
GLA_TAU = 16.0
GLA_CHUNK = 64
DIFF_WIDTH = D_MIX // 4
DIFF_HEADS = 4
DIFF_DV = DIFF_WIDTH // DIFF_HEADS
DIFF_DQK = DIFF_DV // 2
Q_BLOCK = 128
N_BUCKETS = 32
MAX_DISTANCE = 128
SSD_WIDTH = D_MIX // 2
SSD_HEADDIM = 64
SSD_HEADS = SSD_WIDTH // SSD_HEADDIM
SSD_GROUPS = 2
SSD_STATE = 128
SSD_CONV = 5
SSD_CHUNK = 128
SSD_CONV_DIM = SSD_WIDTH + 2 * SSD_GROUPS * SSD_STATE

IN_SIZES = (
    GLA_HEADS * GLA_DK, GLA_HEADS * GLA_DK, GLA_WIDTH, GLA_WIDTH, 2 * GLA_RANK,
    2 * DIFF_HEADS * DIFF_DQK, 2 * DIFF_HEADS * DIFF_DQK, DIFF_WIDTH, DIFF_WIDTH,
    SSD_WIDTH, SSD_CONV_DIM, 2 * SSD_HEADS,
)
IN_TOTAL = sum(IN_SIZES)

kernel_name = 'hybrid_gla_diffattn_ssd_encoder'


def rmsnorm(x, w):
    xf = x.astype(jnp.float32)
    y = xf * lax.rsqrt(jnp.mean(xf * xf, axis=-1, keepdims=True) + EPS)
    return (y * w.astype(jnp.float32)).astype(x.dtype)


def t5_bucket(rel):
    nb = N_BUCKETS // 2
    max_exact = nb // 2
    ret = jnp.where(rel > 0, nb, 0)
    n = jnp.abs(rel)
    nf = jnp.maximum(n, 1).astype(jnp.float32)
    large = max_exact + (jnp.log(nf / max_exact) / math.log(MAX_DISTANCE / max_exact)
                         * (nb - max_exact)).astype(jnp.int32)
    large = jnp.minimum(large, nb - 1)
    return ret + jnp.where(n < max_exact, n, large)


def bidirectional(scan_fn, fwd_args, bwd_args, chunk):
    pad = chunk - N_META
    L = fwd_args[0].shape[1]

    def padseq(t, front):
        widths = [(0, 0)] * t.ndim
        widths[1] = (pad, 0) if front else (0, pad)
        return jnp.pad(t, widths)

    y_f = scan_fn(*[padseq(t, True) for t in fwd_args])[:, pad:]
    y_b = scan_fn(*[padseq(jnp.flip(t, 1), False) for t in bwd_args])[:, :L]
    return y_f + jnp.flip(y_b, 1)


def gla_chunk_scan(q, k, v, g):
    f32 = jnp.float32
    Bsz, T, H, DK = q.shape
    DV = v.shape[-1]
    C = GLA_CHUNK
    N = T // C
    q, k, g = [t.astype(f32).reshape(Bsz, N, C, H, DK) for t in (q, k, g)]
    v = v.astype(f32).reshape(Bsz, N, C, H, DV)
    b = jnp.cumsum(g, axis=2)
    b_last = b[:, :, -1]
    q_in = q * jnp.exp(b)
    k_in = k * jnp.exp(-b)
    k_out = k * jnp.exp(b_last[:, :, None] - b)
    mask = jnp.tril(jnp.ones((C, C), dtype=bool))
    att = jnp.where(mask, jnp.einsum('bnthk,bnshk->bnhts', q_in, k_in), 0.0)
    o_intra = jnp.einsum('bnhts,bnshv->bnthv', att, v)
    chunk_state = jnp.einsum('bnshk,bnshv->bnhkv', k_out, v)

    def step(S, inp):
        d, cs = inp
        return S * d[..., None] + cs, S

    S0 = jnp.zeros((Bsz, H, DK, DV), f32)
    _, S_prev = lax.scan(step, S0, (jnp.moveaxis(jnp.exp(b_last), 1, 0),
                                    jnp.moveaxis(chunk_state, 1, 0)))
    S_prev = jnp.moveaxis(S_prev, 0, 1)
    o_inter = jnp.einsum('bnthk,bnhkv->bnthv', q_in, S_prev)
    return (o_intra + o_inter).reshape(Bsz, T, H, DV)


def ssd_chunk_scan(x, dt, a, Bm, Cm):
    f32 = jnp.float32
    Bsz, T, H, P = x.shape
    G, N = Bm.shape[2], Bm.shape[3]
    R = H // G
    C = SSD_CHUNK
    Nc = T // C
    xd = (x.astype(f32) * dt.astype(f32)[..., None]).reshape(Bsz, Nc, C, G, R, P)
    a = a.astype(f32).reshape(Bsz, Nc, C, G, R)
    Bm = Bm.astype(f32).reshape(Bsz, Nc, C, G, N)
    Cm = Cm.astype(f32).reshape(Bsz, Nc, C, G, N)
    acs = jnp.cumsum(a, axis=2)
    acs_last = acs[:, :, -1]
    causal = jnp.tril(jnp.ones((C, C), dtype=bool))[:, :, None, None]
    seg = acs[:, :, :, None] - acs[:, :, None, :]
    decay_ts = jnp.exp(jnp.where(causal, seg, -jnp.inf))
    scores = jnp.einsum('bctgn,bcsgn->bctsg', Cm, Bm)[..., None] * decay_ts
    y_diag = jnp.einsum('bctsgr,bcsgrp->bctgrp', scores, xd)
    to_end = jnp.exp(acs_last[:, :, None] - acs)
    states = jnp.einsum('bcsgn,bcsgrp->bcgrpn', Bm, xd * to_end[..., None])

    def step(S, inp):
        d, st = inp
        return S * d[..., None, None] + st, S

    S0 = jnp.zeros((Bsz, G, R, P, N), f32)
    _, S_prev = lax.scan(step, S0, (jnp.moveaxis(jnp.exp(acs_last), 1, 0),
                                    jnp.moveaxis(states, 1, 0)))
    S_prev = jnp.moveaxis(S_prev, 0, 1)
    y_off = jnp.einsum('bctgn,bcgrpn->bctgrp', Cm, S_prev) * jnp.exp(acs)[..., None]
    return (y_diag + y_off).reshape(Bsz, T, H, P)


def depthwise_conv(x, w, b):
    K, Ch = w.shape
    y = lax.conv_general_dilated(x, w[:, None, :].astype(x.dtype), window_strides=(1,),
                                 padding=[((K - 1) // 2, K // 2)],
                                 dimension_numbers=('NWC', 'WIO', 'NWC'),
                                 feature_group_count=Ch)
    return y + b.astype(x.dtype)


def diff_attention(q, k, v, lam, lambda_init, rel_bias, sub_w):
    Bsz, L, H = q.shape[0], q.shape[1], q.shape[2]
    DV = v.shape[-1]
    n_blocks = -(-L // Q_BLOCK)
    Lq = n_blocks * Q_BLOCK
    qp = jnp.pad(q, ((0, 0), (0, Lq - L), (0, 0), (0, 0), (0, 0)))
    qb = jnp.moveaxis(qp.reshape(Bsz, n_blocks, Q_BLOCK, H, 2, DIFF_DQK), 1, 0)
    k_pos = jnp.arange(L, dtype=jnp.int32)
    scale = DIFF_DQK ** -0.5

    def block(args):
        qblk, start = args
        q_pos = start + jnp.arange(Q_BLOCK, dtype=jnp.int32)
        bias = rel_bias[t5_bucket(k_pos[None, :] - q_pos[:, None])]
        bias = jnp.transpose(bias, (2, 0, 1)).astype(jnp.float32)
        s = jnp.einsum('bqhcd,bkhcd->bhcqk', qblk, k).astype(jnp.float32) * scale
        p = jax.nn.softmax(s + bias[None, :, None], axis=-1)
        w = p[:, :, 0] - lam * p[:, :, 1]
        return jnp.einsum('bhqk,bkhv->bqhv', w.astype(v.dtype), v)

    starts = jnp.arange(n_blocks, dtype=jnp.int32) * Q_BLOCK
    out = lax.map(block, (qb, starts))
    out = jnp.moveaxis(out, 0, 1).reshape(Bsz, Lq, H, DV)[:, :L]
    return rmsnorm(out, sub_w) * (1.0 - lambda_init)


def hybrid_layer(h, norm_w, w_in, w_out, gla_wa2, gla_ba, gla_norm_w, diff_lambda,
                 diff_norm_w, conv_w, conv_b, ssd_A_log, ssd_dt_bias, ssd_D, ssd_norm_w,
                 rel_bias, lambda_init):
    Bsz, L, _ = h.shape
    dt_ = h.dtype
    u = rmsnorm(h, norm_w)
    proj = u @ w_in.astype(dt_)
    split_idx = np.cumsum(IN_SIZES)[:-1].tolist()
    (gq, gk, gv, ggate, gcode, dq, dk, dv, dgate, z, xbc, dt_raw) = jnp.split(proj, split_idx, axis=-1)

    gq = gq.reshape(Bsz, L, GLA_HEADS, GLA_DK) * (GLA_DK ** -0.5)
    gk = gk.reshape(Bsz, L, GLA_HEADS, GLA_DK)
    gv = gv.reshape(Bsz, L, GLA_HEADS, GLA_DV)
    gcode = gcode.reshape(Bsz, L, 2, GLA_RANK).astype(jnp.float32)
    glog = jax.nn.log_sigmoid(jnp.einsum('blzr,zrk->blzk', gcode, gla_wa2.astype(jnp.float32))
                              + gla_ba.astype(jnp.float32)) / GLA_TAU
    glog = glog.reshape(Bsz, L, 2, GLA_HEADS, GLA_DK)
    o = bidirectional(gla_chunk_scan, (gq, gk, gv, glog[:, :, 0]), (gq, gk, gv, glog[:, :, 1]), GLA_CHUNK)
    o_gla = rmsnorm(o, gla_norm_w).astype(dt_).reshape(Bsz, L, GLA_WIDTH) * jax.nn.silu(ggate)

    dq = dq.reshape(Bsz, L, DIFF_HEADS, 2, DIFF_DQK)
    dk = dk.reshape(Bsz, L, DIFF_HEADS, 2, DIFF_DQK)
    dv = dv.reshape(Bsz, L, DIFF_HEADS, DIFF_DV)
    lp = diff_lambda.astype(jnp.float32)
    lam = jnp.exp(jnp.sum(lp[0] * lp[1])) - jnp.exp(jnp.sum(lp[2] * lp[3])) + lambda_init
    o = diff_attention(dq, dk, dv, lam, lambda_init, rel_bias, diff_norm_w)
    o_diff = o.astype(dt_).reshape(Bsz, L, DIFF_WIDTH) * jax.nn.silu(dgate)

    xbc = jax.nn.silu(depthwise_conv(xbc, conv_w, conv_b))
    xs, Bm, Cm = jnp.split(xbc, [SSD_WIDTH, SSD_WIDTH + SSD_GROUPS * SSD_STATE], axis=-1)
    xs = xs.reshape(Bsz, L, SSD_HEADS, SSD_HEADDIM)
    Bm = Bm.reshape(Bsz, L, SSD_GROUPS, SSD_STATE)
    Cm = Cm.reshape(Bsz, L, SSD_GROUPS, SSD_STATE)
    dt = jax.nn.softplus(dt_raw.reshape(Bsz, L, 2, SSD_HEADS).astype(jnp.float32)
                         + ssd_dt_bias.astype(jnp.float32))
    a = dt * (-jnp.exp(ssd_A_log.astype(jnp.float32)))
    y = bidirectional(ssd_chunk_scan, (xs, dt[:, :, 0], a[:, :, 0], Bm, Cm),
                      (xs, dt[:, :, 1], a[:, :, 1], Bm, Cm), SSD_CHUNK)
    y = y + xs.astype(jnp.float32) * ssd_D.astype(jnp.float32)[:, None]
    y = y.reshape(Bsz, L, SSD_WIDTH) * jax.nn.silu(z.astype(jnp.float32))
    y = rmsnorm(y.reshape(Bsz, L, SSD_GROUPS, SSD_WIDTH // SSD_GROUPS),
                ssd_norm_w.reshape(SSD_GROUPS, SSD_WIDTH // SSD_GROUPS))
    o_ssd = y.reshape(Bsz, L, SSD_WIDTH).astype(dt_)

    mix = jnp.concatenate([o_gla, o_diff, o_ssd], axis=-1)
    return h + mix @ w_out.astype(dt_)


def setup_inputs(seed: int = 0) -> dict:
    key = jax.random.key(seed)
    ks = jax.random.split(key, 20)
    f32 = jnp.float32
    nrm = lambda k, s: jax.random.normal(k, s, f32)
    x = nrm(ks[0], (BATCH, SEQ, D_MODEL))
    meta_tokens = nrm(ks[1], (N_META, D_MODEL))
    rel_bias = 0.5 * nrm(ks[2], (N_BUCKETS, DIFF_HEADS))
    final_norm_w = 1.0 + 0.02 * nrm(ks[3], (D_MODEL,))
    norm_w = 1.0 + 0.02 * nrm(ks[4], (DEPTH, D_MODEL))
    w_in = nrm(ks[5], (DEPTH, D_MODEL, IN_TOTAL)) * D_MODEL ** -0.5
    w_out = nrm(ks[6], (DEPTH, D_MIX, D_MODEL)) * (0.5 * D_MIX ** -0.5)
    gla_wa2 = nrm(ks[7], (DEPTH, 2, GLA_RANK, GLA_HEADS * GLA_DK)) * GLA_RANK ** -0.5
    gla_ba = 0.1 * nrm(ks[8], (DEPTH, 2, GLA_HEADS * GLA_DK))
    gla_norm_w = 1.0 + 0.02 * nrm(ks[9], (DEPTH, GLA_DV))
    diff_lambda = 0.1 * nrm(ks[10], (DEPTH, 4, DIFF_DQK))
    diff_norm_w = 1.0 + 0.02 * nrm(ks[11], (DEPTH, DIFF_DV))
    conv_w = nrm(ks[12], (DEPTH, SSD_CONV, SSD_CONV_DIM)) * SSD_CONV ** -0.5
    conv_b = 0.01 * nrm(ks[13], (DEPTH, SSD_CONV_DIM))
    ssd_A_log = jnp.log(jax.random.uniform(ks[14], (DEPTH, 2, SSD_HEADS), f32, 1.0, 16.0))
    dt0 = jnp.exp(jax.random.uniform(ks[15], (DEPTH, 2, SSD_HEADS), f32,
                                     math.log(1e-3), math.log(1e-1)))
    ssd_dt_bias = dt0 + jnp.log(-jnp.expm1(-dt0))
    ssd_D = 1.0 + 0.1 * nrm(ks[16], (DEPTH, SSD_HEADS))
    ssd_norm_w = 1.0 + 0.02 * nrm(ks[17], (DEPTH, SSD_WIDTH))
    return {'x': x, 'meta_tokens': meta_tokens, 'rel_bias': rel_bias, 'final_norm_w': final_norm_w,
            'norm_w': norm_w, 'w_in': w_in, 'w_out': w_out, 'gla_wa2': gla_wa2, 'gla_ba': gla_ba,
            'gla_norm_w': gla_norm_w, 'diff_lambda': diff_lambda, 'diff_norm_w': diff_norm_w,
            'conv_w': conv_w, 'conv_b': conv_b, 'ssd_A_log': ssd_A_log, 'ssd_dt_bias': ssd_dt_bias,
            'ssd_D': ssd_D, 'ssd_norm_w': ssd_norm_w}


def reference(x, meta_tokens, rel_bias, final_norm_w, norm_w, w_in, w_out, gla_wa2, gla_ba,
              gla_norm_w, diff_lambda, diff_norm_w, conv_w, conv_b, ssd_A_log, ssd_dt_bias,
              ssd_D, ssd_norm_w):
    Bsz = x.shape[0]
    meta = jnp.broadcast_to(meta_tokens[None].astype(x.dtype), (Bsz, N_META, D_MODEL))
    h = jnp.concatenate([meta, x], axis=1)
    for l in range(DEPTH):
        lambda_init = 0.8 - 0.6 * math.exp(-0.3 * l)
        h = hybrid_layer(h, norm_w[l], w_in[l], w_out[l], gla_wa2[l], gla_ba[l], gla_norm_w[l],
                         diff_lambda[l], diff_norm_w[l], conv_w[l], conv_b[l], ssd_A_log[l],
                         ssd_dt_bias[l], ssd_D[l], ssd_norm_w[l], rel_bias, lambda_init)
    h = rmsnorm(h, final_norm_w)
    return h[:, N_META:]
```

```python
import contextlib
import math
import numpy as np
import concourse.bass as bass
import concourse.mybir as mybir
from concourse.bass_utils import run_bass_kernel_spmd

F32 = mybir.dt.float32
BF16 = mybir.dt.bfloat16
AF = mybir.ActivationFunctionType
ALU = mybir.AluOpType
AX = mybir.AxisListType

D = 2048
L = 4112
NT = 33
LP = NT * 128
PADR = 112
DEPTH = 4
INTOT = 6208
EPS = 1e-6
NEG = -30000.0

C_GQ, C_GK, C_GV, C_GG, C_GC = 0, 256, 512, 1024, 1536
C_DQ, C_DK, C_DV, C_DG = 1568, 2080, 2592, 3104
C_Z, C_XBC, C_DT = 3616, 4640, 6176


class Tl:
    def __init__(self, name, h):
        self.name = name
        self.h = h
        self.wf = {}
        self.wp = {}
        self.r = {}
        self.dsem = None

    def __getitem__(self, k):
        return self.h[k]


def _mx(d, k, v):
    if d.get(k, 0) < v:
        d[k] = v


class KB:
    ENG = ('pe', 'act', 'dve', 'pool', 'sp')

    def __init__(self):
        self.nc = bass.Bass("TRN2", target_bir_lowering=False)
        nc = self.nc
        self.engs = dict(pe=nc.tensor, act=nc.scalar, dve=nc.vector, pool=nc.gpsimd, sp=nc.sync)
        self.root = contextlib.ExitStack()
        self.sems = {}
        self.cnt = {}
        for e in self.ENG:
            self.sems[e] = self.root.enter_context(nc.semaphore("s_" + e))
            self.cnt[e] = 0
        self.sems['bar'] = self.root.enter_context(nc.semaphore("s_bar"))
        self.cnt['bar'] = 0
        self.ndsem = 84
        self.free_dsem = []
        for i in range(self.ndsem):
            nm = "d%d" % i
            self.sems[nm] = self.root.enter_context(nc.semaphore(nm))
            self.cnt[nm] = 0
            self.free_dsem.append(nm)
        self.waited = {e: {} for e in self.ENG}
        self.scopes = []
        self.uid = 0
        self.ninstr = 0

    def push(self):
        self.scopes.append((contextlib.ExitStack(), []))

    def pop(self):
        self.barrier()
        es, tiles = self.scopes.pop()
        for t in tiles:
            if t.dsem is not None:
                self.free_dsem.append(t.dsem)
                t.dsem = None
        es.close()

    def tile(self, shape, dtype, name=None, space='sbuf'):
        self.uid += 1
        nm = "%s_%d" % (name or 't', self.uid)
        es, tiles = self.scopes[-1]
        if space == 'sbuf':
            h = es.enter_context(self.nc.sbuf_tensor(nm, list(shape), dtype))
        else:
            h = es.enter_context(self.nc.psum_tensor(nm, list(shape), dtype))
        t = Tl(nm, h)
        tiles.append(t)
        return t

    def psum(self, shape, dtype=F32, name=None):
        return self.tile(shape, dtype, name or 'ps', space='psum')

    def dram(self, name, shape, dtype, kind="Internal"):
        return self.nc.dram_tensor(name, list(shape), dtype, kind=kind).ap()

    def _wait(self, eng, need):
        w = self.waited[eng]
        for s, v in need.items():
            if s == 'pe' and eng == 'pe':
                continue
            if w.get(s, 0) >= v:
                continue
            self.engs[eng].wait_ge(self.sems[s], v)
            self.ninstr += 1
            w[s] = v

    def _deps(self, reads, writes, partial):
        need = {}
        for t in reads:
            for d in (t.wf, t.wp):
                for s, v in d.items():
                    _mx(need, s, v)
        for t in writes:
            for s, v in t.r.items():
                _mx(need, s, v)
            for s, v in t.wf.items():
                _mx(need, s, v)
            if not partial:
                for s, v in t.wp.items():
                    _mx(need, s, v)
        return need

    def _reg(self, s, v, reads, writes, partial):
        for t in reads:
            _mx(t.r, s, v)
        for t in writes:
            if partial:
                _mx(t.wp, s, v)
            else:
                t.wf = {s: v}
                t.wp = {}
                t.r = {}

    def op(self, eng, fn, reads=(), writes=(), partial=False):
        self._wait(eng, self._deps(reads, writes, partial))
        ins = fn(self.engs[eng])
        self.cnt[eng] += 1
        ins.then_inc(self.sems[eng], 1)
        self.ninstr += 1
        self._reg(eng, self.cnt[eng], reads, writes, partial)
        return ins

    def dma(self, q, out, in_, reads=(), writes=(), partial=False, **kw):
        self._wait(q, self._deps(reads, writes, partial))
        owner = (list(writes) + list(reads))[0]
        if owner.dsem is None:
            owner.dsem = self.free_dsem.pop()
        s = owner.dsem
        ins = self.engs[q].dma_start(out=out, in_=in_, **kw)
        self.cnt[s] += 16
        ins.then_inc(self.sems[s], 16)
        self.ninstr += 1
        self._reg(s, self.cnt[s], reads, writes, partial)
        return ins

    def dma_dd(self, q, out, in_, **kw):
        if not hasattr(self, '_ddt'):
            self._ddt = Tl('dd', None)
            self._ddt.dsem = self.free_dsem.pop()
        s = self._ddt.dsem
        ins = self.engs[q].dma_start(out=out, in_=in_, **kw)
        self.cnt[s] += 16
        ins.then_inc(self.sems[s], 16)
        self.ninstr += 1

    def barrier(self):
        sp = self.engs['sp']
        for s, v in self.cnt.items():
            if s in ('bar', 'sp') or v == 0:
                continue
            if self.waited['sp'].get(s, 0) >= v:
                continue
            sp.wait_ge(self.sems[s], v)
            self.waited['sp'][s] = v
            self.ninstr += 1
        self.cnt['bar'] += 1
        sp.sem_inc(self.sems['bar'], 1)
        for e in self.ENG:
            if e == 'sp':
                continue
            self.engs[e].wait_ge(self.sems['bar'], self.cnt['bar'])
            for s, v in self.cnt.items():
                self.waited[e][s] = v
        for s, v in self.cnt.items():
            self.waited['sp'][s] = v
        for es, tiles in self.scopes:
            for t in tiles:
                t.wf, t.wp, t.r = {}, {}, {}


def bc_rows(ap_row, n=128):
    return ap_row.to_broadcast([n, ap_row.shape[-1]])


class Model:
    def __init__(self, debug_out=(), layers=DEPTH, phases=None):
        self.k = KB()
        self.debug_out = set(debug_out)
        self.layers = layers
        self.phases = phases
        self.merge_s1 = True
        k = self.k
        ei = lambda n, s: k.dram(n, s, F32, kind="ExternalInput")
        self.x = ei("x", [4096, D])
        self.meta = ei("meta_tokens", [16, D])
        self.rel_bias = ei("rel_bias", [32, 4])
        self.final_norm_w = ei("final_norm_w", [1, D])
        self.norm_w = ei("norm_w", [DEPTH, D])
        self.w_in = ei("w_in", [DEPTH, D, INTOT])
        self.w_out = ei("w_out", [DEPTH, D, D])
        self.gla_wa2 = ei("gla_wa2", [DEPTH, 2, 16, 256])
        self.gla_ba = ei("gla_ba", [DEPTH, 2, 256])
        self.gla_norm_w = ei("gla_norm_w", [DEPTH, 128])
        self.diff_lambda = ei("diff_lambda", [DEPTH, 256])
        self.diff_norm_w = ei("diff_norm_w", [DEPTH, 128])
        self.conv_w = ei("conv_w", [DEPTH, 5, 1536])
        self.conv_b = ei("conv_b", [DEPTH, 1536])
        self.ssd_A_log = ei("ssd_A_log", [DEPTH, 32])
        self.ssd_dt_bias = ei("ssd_dt_bias", [DEPTH, 32])
        self.ssd_D = ei("ssd_D", [DEPTH, 16])
        self.ssd_norm_w = ei("ssd_norm_w", [DEPTH, 1024])
        self.c_ident = ei("c_ident", [128, 128])
        self.c_tri = ei("c_tri", [4, 128, 128])
        self.c_onehot = ei("c_onehot", [32, 3 * 128 + 128])
        self.out = k.dram("out", [4096, D], F32, kind="ExternalOutput")
        self.sc = {}
        sc = self.scr
        sc("H", [LP, D], F32)
        sc("GQT", [256, LP], BF16)
        sc("GKT", [256, LP], BF16)
        sc("GCT", [32, LP], F32)
        sc("DQT", [512, LP], BF16)
        sc("DKT", [512, LP], BF16)
        sc("XBCT", [1536, LP], F32)
        sc("GK", [LP, 256], BF16)
        sc("GV", [LP, 512], BF16)
        sc("GG", [LP, 512], F32)
        sc("DV", [LP, 512], BF16)
        sc("DG", [LP, 512], F32)
        sc("Z", [LP, 1024], F32)
        sc("DT", [LP, 32], F32)
        sc("MIX", [LP, D], BF16)
        sc("XS", [LP, 1024], F32)
        sc("BM", [LP, 256], BF16)
        sc("BT", [256, LP], BF16)
        sc("CT", [256, LP], BF16)
        sc("YF", [LP, 1024], F32)
        sc("OF", [LP, 512], F32)
        sc("WB", [DEPTH, D, INTOT], BF16)
        sc("WOB", [DEPTH, D, D], BF16)

    def scr(self, name, shape, dtype):
        kind = "ExternalOutput" if name in self.debug_out else "Internal"
        self.sc[name] = self.k.dram("sc_" + name, shape, dtype, kind=kind)
        return self.sc[name]

    def want(self, ph):
        return self.phases is None or ph in self.phases

    def phase_init(self):
        k = self.k
        H = self.sc["H"]
        k.push()
        z = k.tile([PADR, D], F32, "zero")
        k.op('dve', lambda e: e.memset(z[:], 0.0), writes=[z])
        k.dma('sp', H[0:PADR, :], z[:], reads=[z])
        k.dma_dd('sp', H[PADR:128, :], self.meta[:, :])
        for i in range(8):
            k.dma_dd('sp' if i % 2 == 0 else 'act', H[128 + i * 512:128 + (i + 1) * 512, :],
                     self.x[i * 512:(i + 1) * 512, :])
        k.pop()

    def load_consts(self):
        k = self.k
        self.identb = k.tile([128, 128], BF16, "identb")
        k.dma('pool', self.identb[:], self.c_ident[:, :], writes=[self.identb])


    def wconv_steps(self, l, bufs):
        k = self.k
        WB, WOB = self.sc["WB"], self.sc["WOB"]
        blocks = []
        c0 = 0
        while c0 < INTOT:
            n = min(512, INTOT - c0)
            blocks.append((self.w_in[l].rearrange("(c p) n -> p c n", p=128)[:, :, c0:c0 + n],
                           WB[l].rearrange("(c p) n -> p c n", p=128)[:, :, c0:c0 + n], n))
            c0 += n
        for j in range(4):
            blocks.append((self.w_out[l].rearrange("(c p) n -> p c n", p=128)[:, :, j * 512:(j + 1) * 512],
                           WOB[l].rearrange("(c p) n -> p c n", p=128)[:, :, j * 512:(j + 1) * 512], 512))
        steps = []
        nb = len(blocks)
        for i in range(nb + 1):
            def f(i=i):
                if i < nb:
                    src, dst, n = blocks[i]
                    w = bufs[i % 2]
                    k.dma('pool', w[:, :, 0:n], src, writes=[w])
                if i >= 1:
                    src, dst, n = blocks[i - 1]
                    w = bufs[(i - 1) % 2]
                    k.dma('sp', dst, w[:, :, 0:n], reads=[w])
            steps.append(f)
        return steps

    def norm_transpose_block(self, src_tiles_loader, nt, uT, nwb, zero_pad_first):
        pass

    def phase_A(self, l):
        k = self.k
        H = self.sc["H"]
        k.push()
        self.load_consts()
        identb = self.identb
        nwb = k.tile([128, D], F32, "nwb")
        k.dma('sp', nwb[:], bc_rows(self.norm_w[l:l + 1, :]), writes=[nwb])
        TB = 8
        uTs = [k.tile([128, 16, TB * 128], BF16, "uT") for _ in range(2)]
        hb = [k.tile([128, D], F32, "h") for _ in range(2)]
        ub = [k.tile([128, D], BF16, "u") for _ in range(2)]
        junk = k.tile([128, D], BF16, "junk")
        ss = [k.tile([128, 2], F32, "ss") for _ in range(2)]
        pst = [k.psum([128, 8, 128], BF16, "pst") for _ in range(2)]
        psm = [k.psum([128, 512], F32, "psm") for _ in range(4)]
        wfm = [k.tile([128, 16, 128], BF16, "wfm") for _ in range(3)]
        wtm = [k.tile([128, 16, 512], BF16, "wtm") for _ in range(2)]
        ofm = [k.tile([128, 1024], F32, "ofm") for _ in range(2)]
        ofmb = [k.tile([128, 1024], BF16, "ofmb") for _ in range(2)]
        otm = [k.tile([128, 512], F32, "otm") for _ in range(2)]
        otmb = [k.tile([128, 512], BF16, "otmb") for _ in range(2)]
        W = self.sc["WB"][l].rearrange("(c p) n -> p c n", p=128)
        sc = self.sc
        fm = []
        for j in range(2):
            fm.append((C_GQ + 128 * j, 128, sc["GQT"], 128 * j, BF16, 0.125))
        for j in range(2):
            fm.append((C_GK + 128 * j, 128, sc["GKT"], 128 * j, BF16, 1.0))
        fm.append((C_GC, 32, sc["GCT"], 0, F32, 1.0))
        for j in range(4):
            fm.append((C_DQ + 128 * j, 128, sc["DQT"], 128 * j, BF16, 0.125))
        for j in range(4):
            fm.append((C_DK + 128 * j, 128, sc["DKT"], 128 * j, BF16, 1.0))
        for j in range(12):
            fm.append((C_XBC + 128 * j, 128, sc["XBCT"], 128 * j, F32, 1.0))
        tm = [(C_GK, 256, sc["GK"], 0, BF16), (C_GV, 512, sc["GV"], 0, BF16), (C_GG, 512, sc["GG"], 0, F32),
              (C_DV, 512, sc["DV"], 0, BF16), (C_DG, 512, sc["DG"], 0, F32), (C_Z, 512, sc["Z"], 0, F32),
              (C_Z + 512, 512, sc["Z"], 512, F32), (C_DT, 32, sc["DT"], 0, F32)]
        blocks = [(0, 1)] + [(1 + 8 * i, 8) for i in range(4)]
        cntr = dict(h=0, pst=0, fm=0, tm=0, psm=0, ofm=0, otm=0)
        def norm_block(bi):
            t0, nt = blocks[bi]
            uT = uTs[bi % 2]
            for ti in range(nt):
                t = t0 + ti
                i = cntr['h'] % 2
                cntr['h'] += 1
                h, u, s2 = hb[i], ub[i], ss[i]
                k.dma('sp', h[:], H[t * 128:(t + 1) * 128, :], writes=[h])
                k.op('act', lambda e: e.activation(out=junk[:], in_=h[:], func=AF.Square, accum_out=s2[:, 0:1]),
                     reads=[h], writes=[junk, s2])
                k.op('dve', lambda e: e.tensor_scalar(s2[:, 1:2], s2[:, 0:1], 1.0 / D, EPS, ALU.mult, ALU.add),
                     reads=[s2], writes=[s2])
                k.op('act', lambda e: e.activation(out=s2[:, 1:2], in_=s2[:, 1:2], func=AF.Ln), reads=[s2], writes=[s2])
                k.op('act', lambda e: e.activation(out=s2[:, 0:1], in_=s2[:, 1:2], func=AF.Exp, scale=-0.5),
                     reads=[s2], writes=[s2])
                k.op('dve', lambda e: e.scalar_tensor_tensor(out=u[:], in0=h[:], scalar=s2[:, 0:1], in1=nwb[:],
                                                              op0=ALU.mult, op1=ALU.mult),
                     reads=[h, s2, nwb], writes=[u])
                if t == 0:
                    k.op('dve', lambda e: e.memset(u[0:PADR, :], 0.0), writes=[u])
                for half in range(2):
                    p = pst[cntr['pst'] % 2]
                    cntr['pst'] += 1
                    for c in range(8):
                        cc = half * 8 + c
                        k.op('pe', lambda e: e.transpose(p[:, c, :], u[:, cc * 128:(cc + 1) * 128], identb[:]),
                             reads=[u, identb], writes=[p], partial=(c > 0))
                    k.op('act' if half == 0 else 'dve',
                         (lambda e: e.copy(out=uT[:, half * 8:half * 8 + 8, ti * 128:(ti + 1) * 128], in_=p[:]))
                         if half == 0 else
                         (lambda e: e.tensor_copy(out=uT[:, half * 8:half * 8 + 8, ti * 128:(ti + 1) * 128], in_=p[:])),
                         reads=[p], writes=[uT], partial=True)
        norm_block(0)
        for bi, (t0, nt) in enumerate(blocks):
            ntok = nt * 128
            uT = uTs[bi % 2]
            def ld_fm(j):
                col0, nr = fm[j][0], fm[j][1]
                w = wfm[j % 3]
                k.dma('pool', w[:, :, 0:nr], W[:, :, col0:col0 + nr], writes=[w])
            if bi == 0:
                ld_fm(0)
                ld_fm(1)
            for j in range(len(fm)):
                if j + 2 < len(fm):
                    ld_fm(j + 2)
                col0, nr, dst, r0, dt, scale = fm[j]
                w = wfm[j % 3]
                ii = cntr['ofm'] % 2
                cntr['ofm'] += 1
                o = ofm[ii] if dt == F32 else ofmb[ii]
                for n0 in range(0, ntok, 512):
                    nn = min(512, ntok - n0)
                    p = psm[cntr['psm'] % 4]
                    cntr['psm'] += 1
                    for c in range(16):
                        k.op('pe', lambda e: e.matmul(p[0:nr, 0:nn], lhsT=w[:, c, 0:nr], rhs=uT[:, c, n0:n0 + nn],
                                                      start=(c == 0), stop=(c == 15)),
                             reads=[w, uT], writes=[p], partial=(c > 0))
                    k.op('act', lambda e: e.activation(out=o[0:nr, n0:n0 + nn], in_=p[0:nr, 0:nn], func=AF.Copy, scale=scale),
                         reads=[p], writes=[o], partial=(n0 > 0))
                k.dma('sp', dst[r0:r0 + nr, t0 * 128:t0 * 128 + ntok], o[0:nr, 0:ntok], reads=[o])
            if bi + 1 < len(blocks):
                norm_block(bi + 1)
            def ld_tm(j):
                col0, ncl = tm[j][0], tm[j][1]
                w = wtm[j % 2]
                k.dma('pool', w[:, :, 0:ncl], W[:, :, col0:col0 + ncl], writes=[w])
            ld_tm(0)
            for j in range(len(tm)):
                if j + 1 < len(tm):
                    ld_tm(j + 1)
                col0, ncl, dst, c0, dt = tm[j]
                w = wtm[j % 2]
                for ti in range(nt):
                    t = t0 + ti
                    p = psm[cntr['psm'] % 4]
                    cntr['psm'] += 1
                    for c in range(16):
                        k.op('pe', lambda e: e.matmul(p[:, 0:ncl], lhsT=uT[:, c, ti * 128:(ti + 1) * 128],
                                                      rhs=w[:, c, 0:ncl], start=(c == 0), stop=(c == 15)),
                             reads=[w, uT], writes=[p], partial=(c > 0))
                    ii = cntr['otm'] % 2
                    cntr['otm'] += 1
                    o = otm[ii] if dt == F32 else otmb[ii]
                    k.op('dve', lambda e: e.tensor_copy(out=o[:, 0:ncl], in_=p[:, 0:ncl]), reads=[p], writes=[o])
                    k.dma('sp', dst[t * 128:(t + 1) * 128, c0:c0 + ncl], o[:, 0:ncl], reads=[o])
                if j == len(tm) - 3 and bi + 1 < len(blocks):
                    ld_fm(0)
                    ld_fm(1)
        k.pop()


    def phase_prep(self):
        k = self.k
        self.scr("RB", [4, 128, 512], F32)
        RB = self.sc["RB"]
        k.push()
        rb = k.tile([32, 4], F32, "rb")
        oh = k.tile([32, 512], F32, "oh")
        k.dma('sp', rb[:], self.rel_bias[:, :], writes=[rb])
        k.dma('sp', oh[:], self.c_onehot[:, :], writes=[oh])
        for h in range(4):
            p = k.psum([128, 512], F32, "prb")
            o = k.tile([128, 512], F32, "orb")
            k.op('pe', lambda e: e.matmul(p[:], lhsT=rb[:, h:h + 1].to_broadcast([32, 128]), rhs=oh[:],
                                          start=True, stop=True), reads=[rb, oh], writes=[p])
            k.op('dve', lambda e: e.tensor_copy(out=o[:], in_=p[:]), reads=[p], writes=[o])
            k.dma('sp', RB[h], o[:], reads=[o])
        wb = [k.tile([128, 16, 512], BF16, "wcv") for _ in range(2)]
        for f in self.wconv_steps(0, wb):
            f()
        k.pop()

    def merge(self, gens):
        gens = list(gens)
        while gens:
            for g in list(gens):
                try:
                    next(g)
                except StopIteration:
                    gens.remove(g)

    def rstd_col(self, out_col, ss_col, n, tmp_col, reads, writes):
        k = self.k
        k.op('dve', lambda e: e.tensor_scalar(tmp_col, ss_col, 1.0 / n, EPS, ALU.mult, ALU.add), reads=reads, writes=writes)
        k.op('act', lambda e: e.activation(out=tmp_col, in_=tmp_col, func=AF.Ln), reads=writes, writes=writes)
        k.op('act', lambda e: e.activation(out=out_col, in_=tmp_col, func=AF.Exp, scale=-0.5), reads=writes, writes=writes)

    def ones_col(self):
        k = self.k
        self.onesc = k.tile([128, 1], F32, "onesc")
        k.op('dve', lambda e: e.memset(self.onesc[:], 1.0), writes=[self.onesc])
        return self.onesc

    def sigmoid_act(self, out, in_, reads, writes):
        k = self.k
        oc = self.onesc
        npart = out.shape[0]
        k.op('act', lambda e: e.activation(out=out, in_=in_, func=AF.Exp, scale=-1.0), reads=reads, writes=writes)
        k.op('act', lambda e: e.activation(out=out, in_=out, func=AF.Ln, bias=oc[0:npart, 0:1], scale=1.0),
             reads=list(writes) + [oc], writes=writes)
        k.op('act', lambda e: e.activation(out=out, in_=out, func=AF.Exp, scale=-1.0), reads=writes, writes=writes)

    def silu_gate(self, out, y, g, tmp, n, reads, writes, eng2='dve'):
        k = self.k
        self.sigmoid_act(tmp, g, reads, writes)
        k.op(eng2, lambda e: e.tensor_mul(tmp, tmp, g), reads=list(reads) + list(writes), writes=writes)
        k.op('dve', lambda e: e.tensor_mul(out, tmp, y), reads=list(reads) + list(writes), writes=writes)

    def phase_C(self, l):
        k = self.k
        sc = self.sc
        lambda_init = 0.8 - 0.6 * math.exp(-0.3 * l)
        k.push()
        self.ones_col()
        RB = sc["RB"]
        identf = k.tile([128, 128], F32, "identf")
        k.dma('sp', identf[:], self.c_ident[:, :], writes=[identf])
        biasT = k.tile([128, 4, 3, 128], F32, "biasT")
        for h in range(4):
            for di, d in enumerate((-1, 0, 1)):
                off = 255 - 128 * d
                src = bass.AP(tensor=RB.tensor, offset=h * 128 * 512 + off, ap=[[511, 128], [1, 128]])
                k.dma('sp', biasT[:, h, di, :], src, writes=[biasT], partial=True)
        cb = k.tile([128, 16], F32, "cb")
        k.op('dve', lambda e: e.memset(cb[:], 0.0), writes=[cb])
        k.op('dve', lambda e: e.memset(cb[0:PADR, 13:14], NEG), writes=[cb])
        k.dma('sp', cb[:, 0:4], bc_rows(self.rel_bias[15:16, :]), writes=[cb], partial=True)
        k.dma('sp', cb[:, 4:8], bc_rows(self.rel_bias[31:32, :]), writes=[cb], partial=True)
        k.op('dve', lambda e: e.tensor_tensor(out=cb[:, 8:12], in0=cb[:, 0:4], in1=cb[:, 13:14].to_broadcast([128, 4]),
                                              op=ALU.add), reads=[cb], writes=[cb])
        lp = k.tile([128, 256], F32, "lp")
        k.dma('sp', lp[:], bc_rows(self.diff_lambda[l:l + 1, :]), writes=[lp])
        lw = k.tile([128, 8], F32, "lw")
        lj = k.tile([128, 64], F32, "lj")
        for i in range(2):
            k.op('dve', lambda e: e.tensor_tensor(out=lj[:], in0=lp[:, 128 * i:128 * i + 64],
                                                  in1=lp[:, 128 * i + 64:128 * i + 128], op=ALU.mult),
                 reads=[lp], writes=[lj])
            k.op('dve', lambda e: e.reduce_sum(out=lw[:, i:i + 1], in_=lj[:], axis=AX.X), reads=[lj], writes=[lw])
        k.op('act', lambda e: e.activation(out=lw[:, 2:4], in_=lw[:, 0:2], func=AF.Exp), reads=[lw], writes=[lw])
        k.op('dve', lambda e: e.tensor_tensor(out=lw[:, 4:5], in0=lw[:, 3:4], in1=lw[:, 2:3], op=ALU.subtract),
             reads=[lw], writes=[lw])
        k.op('dve', lambda e: e.tensor_scalar_add(lw[:, 5:6], lw[:, 4:5], -lambda_init), reads=[lw], writes=[lw])
        nlam = lw[:, 5:6]
        snw = k.tile([128, 128], F32, "snw")
        k.dma('sp', snw[:], bc_rows(self.diff_norm_w[l:l + 1, :]), writes=[snw])
        k.op('dve', lambda e: e.tensor_scalar_mul(snw[:], snw[:], 1.0 - lambda_init), reads=[snw], writes=[snw])

        kT = [k.tile([128, LP], BF16, "kT") for _ in range(2)]
        qT = [k.tile([128, LP], BF16, "qT") for _ in range(2)]
        va = [k.tile([128, NT, 129], BF16, "va") for _ in range(2)]
        pss = [[k.psum([128, 384], F32, "pss") for _ in range(2)] for _ in range(2)]
        acc = [[k.psum([128, 3, 129], F32, "acc") for _ in range(2)] for _ in range(2)]
        pT = [[k.tile([128, 384], BF16, "pT") for _ in range(2)] for _ in range(2)]
        dg = [k.tile([128, 3, 128], F32, "dg") for _ in range(2)]
        sg = [k.tile([128, 3, 128], F32, "sg") for _ in range(2)]
        o1 = [k.tile([128, 3, 128], F32, "o1") for _ in range(2)]
        ot = [k.tile([128, 3, 128], F32, "ot") for _ in range(2)]
        sq = [k.tile([128, 3, 128], F32, "sq") for _ in range(2)]
        ob = [k.tile([128, 3, 128], BF16, "ob") for _ in range(2)]
        sm = [k.tile([128, 8, 3], F32, "sm") for _ in range(2)]

        def load_head(h):
            b = h % 2
            k.dma('sp', kT[b][:], sc["DKT"][h * 128:(h + 1) * 128, :], writes=[kT[b]])
            k.dma('sp', qT[b][:], sc["DQT"][h * 128:(h + 1) * 128, :], writes=[qT[b]])
            k.dma('sp', va[b][:, :, 0:128],
                  sc["DV"][:, h * 128:(h + 1) * 128].rearrange("(t p) c -> p t c", p=128), writes=[va[b]])
            k.op('pool', lambda e: e.memset(va[b][:, :, 128:129], 1.0), writes=[va[b]], partial=True)

        load_head(0)
        ecnt = [0]
        steps = [(h, qb, j) for h in range(4) for qb in range(11) for j in range(NT)]

        def runs_of(qb, j):
            i0 = 3 * qb
            cls = []
            for s_ in range(3):
                d = j - (i0 + s_)
                if d < -1:
                    cls.append(('lo', None))
                elif d > 1:
                    cls.append(('hi', None))
                else:
                    cls.append(('near', d))
            runs = []
            for s_ in range(3):
                if cls[s_][0] != 'near' and runs and runs[-1][2] == cls[s_]:
                    runs[-1][1] = s_ + 1
                else:
                    runs.append([s_, s_ + 1, cls[s_]])
            return runs

        def scores(si):
            h, qb, j = steps[si]
            hb, sb, i0 = h % 2, si % 2, 3 * qb
            for c in range(2):
                p = pss[sb][c]
                first = True
                for (s0, s1, cl) in runs_of(qb, j):
                    near = cl[0] == 'near'
                    k.op('pe', lambda e: e.matmul(p[:, s0 * 128:s1 * 128],
                                                  lhsT=kT[hb][c * 64:(c + 1) * 64, j * 128:(j + 1) * 128],
                                                  rhs=qT[hb][c * 64:(c + 1) * 64, (i0 + s0) * 128:(i0 + s1) * 128],
                                                  start=True, stop=not near),
                         reads=[kT[hb], qT[hb]], writes=[p], partial=not first)
                    first = False
                    if near:
                        k.op('pe', lambda e: e.matmul(p[:, s0 * 128:s1 * 128], lhsT=identf[:],
                                                      rhs=biasT[:, h, cl[1] + 1, :], start=False, stop=True),
                             reads=[identf, biasT], writes=[p], partial=True)

        def exps(si):
            h, qb, j = steps[si]
            sb = si % 2
            for c in range(2):
                p = pss[sb][c]
                first = True
                for (s0, s1, cl) in runs_of(qb, j):
                    if cl[0] == 'near':
                        col = 13 if j == 0 else 12
                    elif cl[0] == 'lo':
                        col = (8 + h) if j == 0 else h
                    else:
                        col = 4 + h
                    k.op('act', lambda e: e.activation(out=pT[sb][c][:, s0 * 128:s1 * 128],
                                                       in_=p[:, s0 * 128:s1 * 128], func=AF.Exp,
                                                       bias=cb[:, col:col + 1], scale=1.0),
                         reads=[p, cb], writes=[pT[sb][c]], partial=not first)
                    first = False

        def pv(si):
            h, qb, j = steps[si]
            hb, sb, ab = h % 2, si % 2, (h * 11 + qb) % 2
            for c in range(2):
                for s_ in range(3):
                    k.op('pe', lambda e: e.matmul(acc[ab][c][:, s_, :], lhsT=pT[sb][c][:, s_ * 128:(s_ + 1) * 128],
                                                  rhs=va[hb][:, j, :], start=(j == 0 and s_ == 0), stop=(j == NT - 1)),
                         reads=[pT[sb][c], va[hb]], writes=[acc[ab][c]], partial=not (j == 0 and s_ == 0))

        def rows3(dram, qb, c0):
            i0 = 3 * qb
            return dram[i0 * 128:(i0 + 3) * 128, c0:c0 + 128].rearrange("(a p) c -> p a c", p=128)

        def gate_pre(h, qb):
            eb = (h * 11 + qb) % 2
            k.dma('sp', dg[eb][:], rows3(sc["DG"], qb, h * 128), writes=[dg[eb]])
            self.sigmoid_act(sg[eb][:], dg[eb][:], [dg[eb]], [sg[eb]])
            k.op('pool', lambda e: e.tensor_mul(sg[eb][:], sg[eb][:], dg[eb][:]), reads=[sg[eb], dg[eb]], writes=[sg[eb]])

        def epi1(h, qb):
            ab = (h * 11 + qb) % 2
            eb = ab
            a0, a1 = acc[ab][0], acc[ab][1]
            m = sm[eb]
            k.op('dve', lambda e: e.reciprocal(m[:, 0, :], a0[:, :, 128]), reads=[a0], writes=[m])
            k.op('dve', lambda e: e.reciprocal(m[:, 1, :], a1[:, :, 128]), reads=[a1], writes=[m])
            k.op('dve', lambda e: e.tensor_tensor(out=m[:, 2, :], in0=m[:, 1, :], in1=nlam.to_broadcast([128, 3]), op=ALU.mult),
                 reads=[m, lw], writes=[m])
            k.op('dve', lambda e: e.tensor_tensor(out=o1[eb][:], in0=a0[:, :, 0:128],
                                                  in1=m[:, 0, :].unsqueeze(2).to_broadcast([128, 3, 128]), op=ALU.mult),
                 reads=[a0, m], writes=[o1[eb]])
            k.op('dve', lambda e: e.tensor_tensor(out=ot[eb][:], in0=a1[:, :, 0:128],
                                                  in1=m[:, 2, :].unsqueeze(2).to_broadcast([128, 3, 128]), op=ALU.mult),
                 reads=[a1, m], writes=[ot[eb]])
            k.op('pool', lambda e: e.tensor_tensor(out=ot[eb][:], in0=ot[eb][:], in1=o1[eb][:], op=ALU.add),
                 reads=[ot[eb], o1[eb]], writes=[ot[eb]])
            k.op('pool', lambda e: e.tensor_tensor(out=sq[eb][:], in0=ot[eb][:], in1=ot[eb][:], op=ALU.mult),
                 reads=[ot[eb]], writes=[sq[eb]])
            k.op('dve', lambda e: e.reduce_sum(out=m[:, 3, :], in_=sq[eb][:], axis=AX.X), reads=[sq[eb]], writes=[m])
            k.op('dve', lambda e: e.tensor_scalar(m[:, 4, :], m[:, 3, :], 1.0 / 128, EPS, ALU.mult, ALU.add), reads=[m], writes=[m])

        def epi2(h, qb):
            eb = (h * 11 + qb) % 2
            m = sm[eb]
            k.op('act', lambda e: e.activation(out=m[:, 5, :], in_=m[:, 4, :], func=AF.Ln), reads=[m], writes=[m])
            k.op('act', lambda e: e.activation(out=m[:, 6, :], in_=m[:, 5, :], func=AF.Exp, scale=-0.5), reads=[m], writes=[m])
            k.op('dve', lambda e: e.tensor_tensor(out=ot[eb][:], in0=ot[eb][:],
                                                  in1=m[:, 6, :].unsqueeze(2).to_broadcast([128, 3, 128]), op=ALU.mult),
                 reads=[ot[eb], m], writes=[ot[eb]])
            k.op('pool', lambda e: e.tensor_tensor(out=ot[eb][:], in0=ot[eb][:],
                                                   in1=snw[:].unsqueeze(1).to_broadcast([128, 3, 128]), op=ALU.mult),
                 reads=[ot[eb], snw], writes=[ot[eb]])
            k.op('dve', lambda e: e.tensor_tensor(out=ob[eb][:], in0=ot[eb][:], in1=sg[eb][:], op=ALU.mult),
                 reads=[ot[eb], sg[eb]], writes=[ob[eb]])
            k.dma('pool', rows3(sc["MIX"], qb, 512 + h * 128), ob[eb][:], reads=[ob[eb]])

        cv = []
        if l + 1 < self.layers:
            wb = [k.tile([128, 16, 512], BF16, "wcv") for _ in range(2)]
            cv = self.wconv_steps(l + 1, wb)
        pend = []
        scores(0)
        for si in range(len(steps)):
            if cv and si % 60 == 30:
                cv.pop(0)()
            if si + 1 < len(steps):
                scores(si + 1)
            exps(si)
            pv(si)
            h, qb, j = steps[si]
            if qb == 0 and j == 0 and h + 1 < 4:
                load_head(h + 1)
            if j == 4:
                gate_pre(h, qb)
            if j == 8 and pend:
                epi2(*pend.pop(0))
            if j == NT - 1:
                epi1(h, qb)
                pend.append((h, qb))
        while pend:
            epi2(*pend.pop(0))
        while cv:
            cv.pop(0)()
        k.pop()

    def phase_S1(self, l):
        k = self.k
        sc = self.sc
        k.push()
        self.ones_col()
        identf = k.tile([128, 128], F32, "identf")
        k.dma('sp', identf[:], self.c_ident[:, :], writes=[identf])
        identb = k.tile([128, 128], BF16, "identb")
        k.dma('pool', identb[:], self.c_ident[:, :], writes=[identb])
        cw = k.tile([128, 12, 5], F32, "cw")
        for jj in range(5):
            k.dma('sp', cw[:, :, jj], self.conv_w[l, jj, :].rearrange("(c p) -> p c", p=128), writes=[cw], partial=True,
                  allow_slow_non_contiguous=True)
        cbv = k.tile([128, 12], F32, "cbv")
        k.dma('sp', cbv[:], self.conv_b[l, :].rearrange("(c p) -> p c", p=128), writes=[cbv],
              allow_slow_non_contiguous=True)
        xin = [k.tile([128, LP + 4], F32, "xin") for _ in range(2)]
        for b in range(2):
            k.op('dve', lambda e: e.memset(xin[b][:, 0:2], 0.0), writes=[xin[b]])
            k.op('dve', lambda e: e.memset(xin[b][:, LP + 2:LP + 4], 0.0), writes=[xin[b]], partial=True)
        acc = [k.tile([128, LP], F32, "cacc") for _ in range(2)]
        et = [k.tile([128, LP], F32, "cet") for _ in range(2)]
        yb = [k.tile([128, LP], BF16, "cyb") for _ in range(2)]
        pt = [k.psum([128, 4, 128], F32, "cpt") for _ in range(2)]
        ptb = [k.psum([128, 8, 128], BF16, "cptb") for _ in range(2)]
        xo = [k.tile([128, 4, 128], F32, "cxo") for _ in range(2)]
        bo = [k.tile([128, 8, 128], BF16, "cbo") for _ in range(2)]
        cn = dict(p=0, o=0)
        for j in range(12):
            b = j % 2
            eng = 'pool'
            xi, ac, ee = xin[b], acc[b], et[b]
            k.dma('sp', xi[:, 2:LP + 2], sc["XBCT"][j * 128:(j + 1) * 128, :], writes=[xi], partial=True)
            k.op('dve', lambda e: e.tensor_scalar(ac[:], xi[:, 0:LP], cw[:, j, 0:1], cbv[:, j:j + 1], ALU.mult, ALU.add),
                 reads=[xi, cw, cbv], writes=[ac])
            for jj in range(1, 5):
                k.op('dve', lambda e: e.scalar_tensor_tensor(out=ac[:], in0=xi[:, jj:jj + LP], scalar=cw[:, j, jj:jj + 1],
                                                           in1=ac[:], op0=ALU.mult, op1=ALU.add),
                     reads=[xi, cw, ac], writes=[ac])
            self.sigmoid_act(ee[:], ac[:], [ac], [ee])
            if j < 8:
                k.op(eng, lambda e: e.tensor_mul(ac[:], ac[:], ee[:]), reads=[ac, ee], writes=[ac])
                k.op(eng, lambda e: e.memset(ac[:, 0:PADR], 0.0), writes=[ac])
                for t0 in range(0, NT, 4):
                    n = min(4, NT - t0)
                    p = pt[cn['p'] % 2]
                    o = xo[cn['p'] % 2]
                    cn['p'] += 1
                    for i in range(n):
                        k.op('pe', lambda e: e.transpose(p[:, i, :], ac[:, (t0 + i) * 128:(t0 + i + 1) * 128], identf[:]),
                             reads=[ac, identf], writes=[p], partial=(i > 0))
                    k.op('act', lambda e: e.copy(out=o[:, 0:n, :], in_=p[:, 0:n, :]), reads=[p], writes=[o])
                    k.dma('pool', sc["XS"][t0 * 128:(t0 + n) * 128, j * 128:(j + 1) * 128].rearrange("(a p) c -> p a c", p=128),
                          o[:, 0:n, :], reads=[o])
            else:
                y = yb[b]
                k.op(eng, lambda e: e.tensor_mul(y[:], ac[:], ee[:]), reads=[ac, ee], writes=[y])
                k.op(eng, lambda e: e.memset(y[:, 0:PADR], 0.0), writes=[y])
                if j < 10:
                    g = j - 8
                    k.dma('pool', sc["BT"][g * 128:(g + 1) * 128, :], y[:], reads=[y])
                    for t0 in range(0, NT, 8):
                        n = min(8, NT - t0)
                        p = ptb[cn['o'] % 2]
                        o = bo[cn['o'] % 2]
                        cn['o'] += 1
                        for i in range(n):
                            k.op('pe', lambda e: e.transpose(p[:, i, :], y[:, (t0 + i) * 128:(t0 + i + 1) * 128], identb[:]),
                                 reads=[y, identb], writes=[p], partial=(i > 0))
                        k.op('act', lambda e: e.copy(out=o[:, 0:n, :], in_=p[:, 0:n, :]), reads=[p], writes=[o])
                        k.dma('pool', sc["BM"][t0 * 128:(t0 + n) * 128, g * 128:(g + 1) * 128].rearrange("(a p) c -> p a c", p=128),
                              o[:, 0:n, :], reads=[o])
                else:
                    g = j - 10
                    k.dma('pool', sc["CT"][g * 128:(g + 1) * 128, :], y[:], reads=[y])
        k.pop()

    def s1_gen(self, l):
        k = self.k
        sc = self.sc
        identf = k.tile([128, 128], F32, "identf")
        k.dma('sp', identf[:], self.c_ident[:, :], writes=[identf])
        identb = k.tile([128, 128], BF16, "identb")
        k.dma('pool', identb[:], self.c_ident[:, :], writes=[identb])
        cw = k.tile([128, 12, 5], F32, "cw")
        for jj in range(5):
            k.dma('sp', cw[:, :, jj], self.conv_w[l, jj, :].rearrange("(c p) -> p c", p=128), writes=[cw], partial=True,
                  allow_slow_non_contiguous=True)
        cbv = k.tile([128, 12], F32, "cbv")
        k.dma('sp', cbv[:], self.conv_b[l, :].rearrange("(c p) -> p c", p=128), writes=[cbv],
              allow_slow_non_contiguous=True)
        xin = [k.tile([128, LP + 4], F32, "xin") for _ in range(2)]
        for b in range(2):
            k.op('dve', lambda e: e.memset(xin[b][:, 0:2], 0.0), writes=[xin[b]])
            k.op('dve', lambda e: e.memset(xin[b][:, LP + 2:LP + 4], 0.0), writes=[xin[b]], partial=True)
        acc = [k.tile([128, LP], F32, "cacc") for _ in range(1)] * 2
        et = [k.tile([128, LP], F32, "cet") for _ in range(1)] * 2
        yb = [k.tile([128, LP], BF16, "cyb") for _ in range(1)] * 2
        pt = [k.psum([128, 4, 128], F32, "cpt") for _ in range(1)] * 2
        ptb = [k.psum([128, 8, 128], BF16, "cptb") for _ in range(1)] * 2
        xo = [k.tile([128, 4, 128], F32, "cxo") for _ in range(2)]
        bo = [k.tile([128, 8, 128], BF16, "cbo") for _ in range(2)]
        cn = dict(p=0, o=0)
        for j in range(12):
            b = j % 2
            eng = 'pool'
            xi, ac, ee = xin[b], acc[b], et[b]
            yield
            k.dma('sp', xi[:, 2:LP + 2], sc["XBCT"][j * 128:(j + 1) * 128, :], writes=[xi], partial=True)
            yield
            k.op('dve', lambda e: e.tensor_scalar(ac[:], xi[:, 0:LP], cw[:, j, 0:1], cbv[:, j:j + 1], ALU.mult, ALU.add),
                 reads=[xi, cw, cbv], writes=[ac])
            for jj in range(1, 5):
                yield
                k.op('dve', lambda e: e.scalar_tensor_tensor(out=ac[:], in0=xi[:, jj:jj + LP], scalar=cw[:, j, jj:jj + 1],
                                                           in1=ac[:], op0=ALU.mult, op1=ALU.add),
                     reads=[xi, cw, ac], writes=[ac])
            yield
            self.sigmoid_act(ee[:], ac[:], [ac], [ee])
            if j < 8:
                yield
                k.op(eng, lambda e: e.tensor_mul(ac[:], ac[:], ee[:]), reads=[ac, ee], writes=[ac])
                yield
                k.op(eng, lambda e: e.memset(ac[:, 0:PADR], 0.0), writes=[ac])
                for t0 in range(0, NT, 4):
                    n = min(4, NT - t0)
                    p = pt[cn['p'] % 2]
                    o = xo[cn['p'] % 2]
                    cn['p'] += 1
                    for i in range(n):
                        k.op('pe', lambda e: e.transpose(p[:, i, :], ac[:, (t0 + i) * 128:(t0 + i + 1) * 128], identf[:]),
                             reads=[ac, identf], writes=[p], partial=(i > 0))
                    yield
                    k.op('act', lambda e: e.copy(out=o[:, 0:n, :], in_=p[:, 0:n, :]), reads=[p], writes=[o])
                    yield
                    k.dma('pool', sc["XS"][t0 * 128:(t0 + n) * 128, j * 128:(j + 1) * 128].rearrange("(a p) c -> p a c", p=128),
                          o[:, 0:n, :], reads=[o])
            else:
                y = yb[b]
                yield
                k.op(eng, lambda e: e.tensor_mul(y[:], ac[:], ee[:]), reads=[ac, ee], writes=[y])
                yield
                k.op(eng, lambda e: e.memset(y[:, 0:PADR], 0.0), writes=[y])
                if j < 10:
                    g = j - 8
                    yield
                    k.dma('pool', sc["BT"][g * 128:(g + 1) * 128, :], y[:], reads=[y])
                    for t0 in range(0, NT, 8):
                        n = min(8, NT - t0)
                        p = ptb[cn['o'] % 2]
                        o = bo[cn['o'] % 2]
                        cn['o'] += 1
                        for i in range(n):
                            k.op('pe', lambda e: e.transpose(p[:, i, :], y[:, (t0 + i) * 128:(t0 + i + 1) * 128], identb[:]),
                                 reads=[y, identb], writes=[p], partial=(i > 0))
                        k.op('act', lambda e: e.copy(out=o[:, 0:n, :], in_=p[:, 0:n, :]), reads=[p], writes=[o])
                        k.dma('pool', sc["BM"][t0 * 128:(t0 + n) * 128, g * 128:(g + 1) * 128].rearrange("(a p) c -> p a c", p=128),
                              o[:, 0:n, :], reads=[o])
                else:
                    g = j - 10
                    yield
                    k.dma('pool', sc["CT"][g * 128:(g + 1) * 128, :], y[:], reads=[y])


    def load_tri(self, scale=None, name="tri"):
        k = self.k
        tri = k.tile([128, 4, 128], F32, name)
        k.dma('sp', tri[:], self.c_tri.rearrange("k s t -> s k t"), writes=[tri])
        if scale is not None:
            k.op('dve', lambda e: e.tensor_scalar_mul(tri[:], tri[:], scale), reads=[tri], writes=[tri])
        return tri

    def phase_S2(self, l):
        k = self.k
        sc = self.sc
        k.push()
        self.ones_col()
        identf = k.tile([128, 128], F32, "identf")
        k.dma('sp', identf[:], self.c_ident[:, :], writes=[identf])
        ones = k.tile([128, 128], F32, "ones")
        k.op('dve', lambda e: e.memset(ones[:], 1.0), writes=[ones])
        tri = self.load_tri()
        negm = k.tile([128, 2, 128], F32, "negm")
        k.op('dve', lambda e: e.tensor_scalar(negm[:], tri[:, 0:2, :], -NEG, NEG, ALU.mult, ALU.add), reads=[tri], writes=[negm])
        dtb = k.tile([128, 32], F32, "dtb")
        k.dma('sp', dtb[:], bc_rows(self.ssd_dt_bias[l:l + 1, :]), writes=[dtb])
        Av = k.tile([128, 32], F32, "Av")
        k.dma('sp', Av[:], bc_rows(self.ssd_A_log[l:l + 1, :]), writes=[Av])
        k.op('act', lambda e: e.activation(out=Av[:], in_=Av[:], func=AF.Exp), reads=[Av], writes=[Av])
        k.op('dve', lambda e: e.tensor_scalar_mul(Av[:], Av[:], -1.0), reads=[Av], writes=[Av])
        Dv = k.tile([128, 16], F32, "Dv")
        k.dma('sp', Dv[:], bc_rows(self.ssd_D[l:l + 1, :]), writes=[Dv])
        nw = k.tile([128, 1024], F32, "snw")
        k.dma('sp', nw[:], bc_rows(self.ssd_norm_w[l:l + 1, :]), writes=[nw])

        xs = [k.tile([128, 16, 64], F32, "xs") for _ in range(4)]
        bm = [k.tile([128, 256], BF16, "bm") for _ in range(4)]
        bt = [k.tile([128, 2, 128], BF16, "bt") for _ in range(4)]
        ct = [k.tile([128, 2, 128], BF16, "ct") for _ in range(4)]
        dtr = [k.tile([128, 32], F32, "dtr") for _ in range(4)]
        zt = [k.tile([128, 1024], F32, "zt") for _ in range(4)]
        yf = [k.tile([128, 1024], F32, "yf") for _ in range(4)]
        sm = [k.tile([128, 8, 16], F32, "ssm") for _ in range(3)]
        xd = [k.tile([128, 16, 64], BF16, "xd") for _ in range(3)]
        xdw = [k.tile([128, 16, 64], BF16, "xdw") for _ in range(3)]
        GT = [k.tile([128, 2, 128], F32, "GT") for _ in range(3)]
        Lt = [k.tile([128, 4, 128], F32, "Lt") for _ in range(2)]
        Mt = [k.tile([128, 4, 128], BF16, "Mt") for _ in range(2)]
        ysb = [k.tile([128, 16, 64], F32, "ysb") for _ in range(2)]
        ytmp = [k.tile([128, 16, 64], F32, "ytmp") for _ in range(2)]
        szt = [k.tile([128, 1024], F32, "szt") for _ in range(4)]
        pre = [k.tile([128, 16, 64], F32, "pre") for _ in range(4)]
        S = k.tile([128, 16, 64], F32, "S")
        Sb = k.tile([128, 16, 64], BF16, "Sb")
        St = k.tile([128, 16, 64], F32, "Stmp")
        ob = [k.tile([128, 1024], BF16, "sob") for _ in range(2)]
        m2 = [k.tile([128, 8], F32, "m2") for _ in range(2)]
        junk = k.tile([128, 512], F32, "sjunk")
        psY = [k.psum([128, 8, 64], F32, "psY") for _ in range(2)]
        psO = [k.psum([128, 8, 64], F32, "psO") for _ in range(2)]
        psS = k.psum([128, 8, 64], F32, "psS")
        psL = [k.psum([128, 4, 128], F32, "psL") for _ in range(2)]
        psM = k.psum([128, 512], F32, "psM")

        def loads(t, dd):
            b3 = t % 4
            r = slice(t * 128, (t + 1) * 128)
            k.dma('sp', xs[b3][:].rearrange("p h d -> p (h d)"), sc["XS"][r, :], writes=[xs[b3]])
            k.dma('sp', bm[b3][:], sc["BM"][r, :], writes=[bm[b3]])
            k.dma('sp', bt[b3][:], sc["BT"][:, r].rearrange("(g n) t -> n g t", g=2), writes=[bt[b3]])
            k.dma('sp', ct[b3][:], sc["CT"][:, r].rearrange("(g n) t -> n g t", g=2), writes=[ct[b3]])
            k.dma('sp', dtr[b3][:], sc["DT"][r, :], writes=[dtr[b3]])
            if dd == 1:
                k.dma('sp', zt[b3][:], sc["Z"][r, :], writes=[zt[b3]])
                k.dma('sp', yf[b3][:], sc["YF"][r, :], writes=[yf[b3]])

        lc = 0
        for dd in range(2):
            order = list(range(NT)) if dd == 0 else list(range(NT - 1, -1, -1))
            k.op('dve', lambda e: e.memset(S[:], 0.0), writes=[S])
            k.op('dve', lambda e: e.memset(Sb[:], 0.0), writes=[Sb])
            def stage1(t, dd=dd):
                b = t % 2
                b3 = t % 4
                bs = t % 3
                m = sm[bs]
                cs = slice(dd * 16, dd * 16 + 16)
                x_, bm_, bt_, ct_ = xs[b3], bm[b3], bt[b3], ct[b3]
                yield
                k.op('dve', lambda e: e.tensor_tensor(out=m[:, 0, :], in0=dtr[b3][:, cs], in1=dtb[:, cs], op=ALU.add),
                     reads=[dtr[b3], dtb], writes=[m])
                yield
                k.op('act', lambda e: e.activation(out=m[:, 0, :], in_=m[:, 0, :], func=AF.Exp), reads=[m], writes=[m])
                yield
                k.op('dve', lambda e: e.tensor_scalar_add(m[:, 0, :], m[:, 0, :], 1.0), reads=[m], writes=[m])
                yield
                k.op('act', lambda e: e.activation(out=m[:, 0, :], in_=m[:, 0, :], func=AF.Ln), reads=[m], writes=[m])
                yield
                k.op('dve', lambda e: e.tensor_tensor(out=m[:, 1, :], in0=m[:, 0, :], in1=Av[:, cs], op=ALU.mult),
                     reads=[m, Av], writes=[m])
                yield
                k.op('pe', lambda e: e.matmul(psM[:, 256:272], lhsT=tri[:, dd, :], rhs=m[:, 1, :], start=True, stop=True),
                     reads=[tri, m], writes=[psM], partial=True)
                yield
                k.op('pe', lambda e: e.matmul(psM[:, 272:288], lhsT=tri[:, 2 + dd, :], rhs=m[:, 1, :], start=True, stop=True),
                     reads=[tri, m], writes=[psM], partial=True)
                yield
                k.op('pe', lambda e: e.matmul(psM[:, 288:304], lhsT=ones[:], rhs=m[:, 1, :], start=True, stop=True),
                     reads=[ones, m], writes=[psM], partial=True)
                yield
                k.op('dve', lambda e: e.tensor_copy(out=m[:, 2, :], in_=psM[:, 256:272]), reads=[psM], writes=[m])
                yield
                k.op('dve', lambda e: e.tensor_scalar_mul(m[:, 3, :], m[:, 2, :], -1.0), reads=[m], writes=[m])
                yield
                k.op('act', lambda e: e.activation(out=m[:, 4:7, :], in_=psM[:, 256:304].rearrange("p (a h) -> p a h", a=3),
                                                   func=AF.Exp), reads=[psM], writes=[m])
                yield
                k.op('dve', lambda e: e.tensor_tensor(out=m[:, 7, :], in0=m[:, 0, :], in1=m[:, 5, :], op=ALU.mult),
                     reads=[m], writes=[m])
                yield
                k.op('pool', lambda e: e.tensor_tensor(out=xd[bs][:], in0=x_[:], in1=m[:, 0, :].unsqueeze(2).to_broadcast([128, 16, 64]),
                                                       op=ALU.mult), reads=[x_, m], writes=[xd[bs]])
                yield
                k.op('pool', lambda e: e.tensor_tensor(out=xdw[bs][:], in0=x_[:], in1=m[:, 7, :].unsqueeze(2).to_broadcast([128, 16, 64]),
                                                       op=ALU.mult), reads=[x_, m], writes=[xdw[bs]])
                yield
                for g in range(2):
                    k.op('pe', lambda e: e.matmul(psM[:, g * 128:(g + 1) * 128], lhsT=bt_[:, g, :], rhs=ct_[:, g, :],
                                                  start=True, stop=True), reads=[bt_, ct_], writes=[psM], partial=True)
                yield
                k.op('act', lambda e: e.copy(out=GT[bs][:].rearrange("p g t -> p (g t)"), in_=psM[:, 0:256]),
                     reads=[psM], writes=[GT[bs]])
                if dd == 1:
                    yield
                    self.sigmoid_act(szt[b3][:], zt[b3][:], [zt[b3]], [szt[b3]])
                    yield
                    k.op('pool', lambda e: e.tensor_mul(szt[b3][:], szt[b3][:], zt[b3][:]), reads=[szt[b3], zt[b3]], writes=[szt[b3]])
                    yield
                    k.op('pool', lambda e: e.tensor_tensor(out=pre[b3][:], in0=x_[:], in1=Dv[:, :].unsqueeze(2).to_broadcast([128, 16, 64]),
                                                           op=ALU.mult), reads=[x_, Dv], writes=[pre[b3]])
            def stage2(t, dd=dd):
                b = t % 2
                b3 = t % 4
                bs = t % 3
                m = sm[bs]
                cs = slice(dd * 16, dd * 16 + 16)
                x_, bm_, bt_, ct_ = xs[b3], bm[b3], bt[b3], ct[b3]
                yield
                for g in range(2):
                    k.op('pe', lambda e: e.matmul(psO[g][:].rearrange("p h d -> p (h d)"), lhsT=ct_[:, g, :],
                                                  rhs=Sb[:, g * 8:(g + 1) * 8, :].rearrange("p h d -> p (h d)"),
                                                  start=True, stop=True), reads=[ct_, Sb], writes=[psO[g]])
                yb_ = ysb[b]
                yt_ = ytmp[b]
                yield
                for g in range(2):
                    hs = slice(g * 8, (g + 1) * 8)
                    k.op('dve', lambda e: e.tensor_tensor(out=yt_[:, hs, :], in0=psO[g][:],
                                                          in1=m[:, 4, hs].unsqueeze(2).to_broadcast([128, 8, 64]), op=ALU.mult),
                         reads=[psO[g], m], writes=[yt_], partial=(g > 0))

                def Lmm(hg):
                    pl = psL[hg % 2]
                    for i in range(4):
                        h = hg * 4 + i
                        k.op('pe', lambda e: e.matmul(pl[:, i, :], lhsT=identf[:], rhs=negm[:, dd, :], start=(i == 0), stop=False),
                             reads=[identf, negm], writes=[pl], partial=(i > 0))
                        k.op('pe', lambda e: e.matmul(pl[:, i, :], lhsT=m[:, 1, h:h + 1].to_broadcast([128, 128]), rhs=tri[:, dd, :],
                                                      start=False, stop=True), reads=[m, tri], writes=[pl], partial=True)

                yield
                Lmm(0)
                yield
                for hg in range(4):
                    if hg + 1 < 4:
                        Lmm(hg + 1)
                    yield
                    g = hg // 2
                    pl = psL[hg % 2]
                    lt = Lt[hg % 2]
                    mt = Mt[hg % 2]
                    k.op('dve', lambda e: e.tensor_tensor(out=lt[:], in0=pl[:],
                                                          in1=m[:, 3, hg * 4:hg * 4 + 4].unsqueeze(2).to_broadcast([128, 4, 128]),
                                                          op=ALU.add), reads=[pl, m], writes=[lt])
                    k.op('act', lambda e: e.activation(out=lt[:], in_=lt[:], func=AF.Exp), reads=[lt], writes=[lt])
                    k.op('pool', lambda e: e.tensor_tensor(out=mt[:], in0=lt[:],
                                                           in1=GT[bs][:, g, :].unsqueeze(1).to_broadcast([128, 4, 128]), op=ALU.mult),
                         reads=[lt, GT[bs]], writes=[mt])
                    for i in range(4):
                        h = hg * 4 + i
                        k.op('pe', lambda e: e.matmul(psY[g][:, h % 8, :], lhsT=mt[:, i, :], rhs=xd[bs][:, h, :],
                                                      start=(h % 8 == 0), stop=True), reads=[mt, xd[bs]], writes=[psY[g]],
                             partial=(h % 8 != 0))
                yield
                for g in range(2):
                    hs = slice(g * 8, (g + 1) * 8)
                    k.op('dve', lambda e: e.tensor_tensor(out=yb_[:, hs, :], in0=psY[g][:], in1=yt_[:, hs, :], op=ALU.add),
                         reads=[psY[g], yt_], writes=[yb_], partial=(g > 0))
                yield
                for g in range(2):
                    hs = slice(g * 8, (g + 1) * 8)
                    k.op('pe', lambda e: e.matmul(psS[:].rearrange("p h d -> p (h d)"), lhsT=bm_[:, g * 128:(g + 1) * 128],
                                                  rhs=xdw[bs][:, hs, :].rearrange("p h d -> p (h d)"), start=True, stop=True),
                         reads=[bm_, xdw[bs]], writes=[psS])
                    k.op('pool', lambda e: e.tensor_tensor(out=St[:, hs, :], in0=S[:, hs, :],
                                                           in1=m[:, 6, hs].unsqueeze(2).to_broadcast([128, 8, 64]), op=ALU.mult),
                         reads=[S, m], writes=[St], partial=(g > 0))
                    k.op('dve', lambda e: e.tensor_tensor(out=S[:, hs, :], in0=psS[:], in1=St[:, hs, :], op=ALU.add),
                         reads=[psS, St], writes=[S], partial=(g > 0))
                yield
                k.op('act', lambda e: e.copy(out=Sb[:], in_=S[:]), reads=[S], writes=[Sb])
                r = slice(t * 128, (t + 1) * 128)
                if dd == 0:
                    k.dma('pool', sc["YF"][r, :], yb_[:].rearrange("p h d -> p (h d)"), reads=[yb_])
            def stage3(t, dd=dd):
                b = t % 2
                b3 = t % 4
                bs = t % 3
                yb_ = ysb[b]
                r = slice(t * 128, (t + 1) * 128)
                yfl = yf[b3][:].rearrange("p (h d) -> p h d", h=16)
                k.op('dve', lambda e: e.tensor_tensor(out=yb_[:], in0=yb_[:], in1=yfl, op=ALU.add),
                     reads=[yb_, yf[b3]], writes=[yb_])
                yield
                k.op('pool', lambda e: e.tensor_tensor(out=yb_[:], in0=yb_[:], in1=pre[b3][:], op=ALU.add),
                     reads=[yb_, pre[b3]], writes=[yb_])
                yield
                yfl2 = yb_[:].rearrange("p h d -> p (h d)")
                k.op('dve', lambda e: e.tensor_tensor(out=yfl2, in0=yfl2, in1=szt[b3][:], op=ALU.mult),
                     reads=[yb_, szt[b3]], writes=[yb_])
                yield
                mm = m2[b]
                for g in range(2):
                    k.op('act', lambda e: e.activation(out=junk[:], in_=yfl2[:, g * 512:(g + 1) * 512], func=AF.Square,
                                                       accum_out=mm[:, g:g + 1]), reads=[yb_], writes=[junk, mm],
                         partial=(g > 0))
                yield
                self.rstd_col(mm[:, 4:6], mm[:, 0:2], 512, mm[:, 2:4], [mm], [mm])
                yield
                o_ = ob[b]
                for g in range(2):
                    k.op('dve',
                         lambda e: e.scalar_tensor_tensor(out=o_[:, g * 512:(g + 1) * 512], in0=yfl2[:, g * 512:(g + 1) * 512],
                                                          scalar=mm[:, 4 + g:5 + g], in1=nw[:, g * 512:(g + 1) * 512],
                                                          op0=ALU.mult, op1=ALU.mult),
                         reads=[yb_, mm, nw], writes=[o_], partial=(g > 0))
                    yield
                k.dma('pool', sc["MIX"][r, 1024:2048], o_[:], reads=[o_])
            loads(order[0], dd)
            loads(order[1], dd)
            loads(order[2], dd)
            for _ in stage1(order[0]):
                pass
            for _ in stage1(order[1]):
                pass
            for oi, t in enumerate(order):
                gens = [stage2(t)]
                if oi + 2 < NT:
                    gens.append(stage1(order[oi + 2]))
                if dd == 1 and oi >= 1:
                    gens.append(stage3(order[oi - 1]))
                self.merge(gens)
                if oi + 3 < NT:
                    loads(order[oi + 3], dd)
            if dd == 1:
                for _ in stage3(order[NT - 1]):
                    pass
            k.barrier()
        k.pop()

    def phase_B(self, l):
        k = self.k
        sc = self.sc
        k.push()
        self.ones_col()
        tri = self.load_tri()
        triS = self.load_tri(scale=-1.0 / 16.0, name="triS")
        wa = k.tile([17, 2, 256], F32, "wa")
        for z in range(2):
            k.dma('sp', wa[0:16, z, :], self.gla_wa2[l, z], writes=[wa], partial=True)
            k.dma('sp', wa[16:17, z, :], self.gla_ba[l, z:z + 1, :], writes=[wa], partial=True)
        gnw = k.tile([128, 128], F32, "gnw")
        k.dma('sp', gnw[:], bc_rows(self.gla_norm_w[l:l + 1, :]), writes=[gnw])
        gca = [k.tile([17, 128], F32, "gca") for _ in range(4)]
        for b in range(4):
            k.op('dve', lambda e: e.memset(gca[b][:], 1.0), writes=[gca[b]])
        qT = [k.tile([64, 4, 128], BF16, "gqT") for _ in range(4)]
        kT = [k.tile([64, 4, 128], BF16, "gkT") for _ in range(4)]
        km = [k.tile([128, 256], BF16, "gkm") for _ in range(4)]
        vt = [k.tile([128, 512], BF16, "gv") for _ in range(4)]
        gg = [k.tile([128, 512], F32, "ggt") for _ in range(4)]
        of = [k.tile([128, 512], F32, "gof") for _ in range(4)]
        sp_ = [k.tile([128, 256], F32, "gsp") for _ in range(3)]
        ebT = [k.tile([64, 4, 128], F32, "ebT") for _ in range(3)]
        enbT = [k.tile([64, 4, 128], F32, "enbT") for _ in range(3)]
        qin = [k.tile([64, 4, 128], BF16, "qin") for _ in range(3)]
        kin = [k.tile([64, 4, 128], BF16, "kin") for _ in range(3)]
        kout = [k.tile([128, 256], BF16, "kout") for _ in range(3)]
        ete = [k.tile([128, 256], F32, "ete") for _ in range(3)]
        att = [k.tile([128, 4, 128], BF16, "att") for _ in range(3)]
        osb = [k.tile([128, 4, 128], F32, "osb") for _ in range(2)]
        otm = [k.tile([128, 4, 128], F32, "gotm") for _ in range(2)]
        ob = [k.tile([128, 512], BF16, "gob") for _ in range(2)]
        m2 = [k.tile([128, 16], F32, "gm2") for _ in range(2)]
        junk = k.tile([128, 128], F32, "gjunk")
        sgt = [k.tile([128, 512], F32, "sgt") for _ in range(4)]
        S = k.tile([64, 4, 128], F32, "gS")
        Sb = k.tile([64, 4, 128], BF16, "gSb")
        psX = k.psum([128, 256], F32, "psX")
        psB = k.psum([64, 4, 128], F32, "psB")
        psE = k.psum([128, 256], F32, "psE")
        psA = k.psum([128, 4, 128], F32, "psA")
        psO = k.psum([128, 4, 128], F32, "gpsO")
        psS = k.psum([64, 4, 128], F32, "gpsS")

        s1g = self.s1_gen(l) if self.merge_s1 else None

        def loads(t, dd):
            b3 = t % 4
            r = slice(t * 128, (t + 1) * 128)
            k.dma('sp', gca[b3][0:16, :], sc["GCT"][dd * 16:(dd + 1) * 16, r], writes=[gca[b3]], partial=True)
            k.dma('sp', qT[b3][:], sc["GQT"][:, r].rearrange("(h k) t -> k h t", h=4), writes=[qT[b3]])
            k.dma('sp', kT[b3][:], sc["GKT"][:, r].rearrange("(h k) t -> k h t", h=4), writes=[kT[b3]])
            k.dma('sp', km[b3][:], sc["GK"][r, :], writes=[km[b3]])
            k.dma('sp', vt[b3][:], sc["GV"][r, :], writes=[vt[b3]])
            if dd == 1:
                k.dma('sp', gg[b3][:], sc["GG"][r, :], writes=[gg[b3]])
                k.dma('sp', of[b3][:], sc["OF"][r, :], writes=[of[b3]])

        for dd in range(2):
            order = list(range(NT)) if dd == 0 else list(range(NT - 1, -1, -1))
            last = 127 if dd == 0 else 0
            k.op('dve', lambda e: e.memset(S[:], 0.0), writes=[S])
            k.op('dve', lambda e: e.memset(Sb[:], 0.0), writes=[Sb])
            def stage1(t, dd=dd, last=last):
                b = t % 2
                b3 = t % 4
                bs = t % 3
                sp = sp_[bs]
                yield
                k.op('pe', lambda e: e.matmul(psX[:], lhsT=gca[b3][0:17, :], rhs=wa[0:17, dd, :], start=True, stop=True),
                     reads=[gca[b3], wa], writes=[psX])
                yield
                k.op('act', lambda e: e.activation(out=sp[:], in_=psX[:], func=AF.Exp, scale=-1.0), reads=[psX], writes=[sp])
                yield
                k.op('dve', lambda e: e.tensor_scalar_add(sp[:], sp[:], 1.0), reads=[sp], writes=[sp])
                yield
                k.op('act', lambda e: e.activation(out=sp[:], in_=sp[:], func=AF.Ln), reads=[sp], writes=[sp])
                yield
                for hd in range(4):
                    k.op('pe', lambda e: e.matmul(psB[:, hd, :], lhsT=sp[:, hd * 64:(hd + 1) * 64], rhs=triS[:, dd, :],
                                                  start=True, stop=True), reads=[sp, triS], writes=[psB], partial=(hd > 0))
                yield
                k.op('pe', lambda e: e.matmul(psE[:], lhsT=triS[:, 2 + dd, :], rhs=sp[:], start=True, stop=True),
                     reads=[sp, triS], writes=[psE])
                yield
                k.op('act', lambda e: e.activation(out=ebT[bs][:], in_=psB[:], func=AF.Exp), reads=[psB], writes=[ebT[bs]])
                yield
                k.op('act', lambda e: e.activation(out=enbT[bs][:], in_=psB[:], func=AF.Exp, scale=-1.0), reads=[psB], writes=[enbT[bs]])
                yield
                k.op('act', lambda e: e.activation(out=ete[bs][:], in_=psE[:], func=AF.Exp), reads=[psE], writes=[ete[bs]])
                yield
                k.op('dve', lambda e: e.tensor_tensor(out=qin[bs][:], in0=qT[b3][:], in1=ebT[bs][:], op=ALU.mult),
                     reads=[qT[b3], ebT[bs]], writes=[qin[bs]])
                yield
                k.op('pool', lambda e: e.tensor_tensor(out=kin[bs][:], in0=kT[b3][:], in1=enbT[bs][:], op=ALU.mult),
                     reads=[kT[b3], enbT[bs]], writes=[kin[bs]])
                yield
                k.op('pool', lambda e: e.tensor_tensor(out=kout[bs][:], in0=km[b3][:], in1=ete[bs][:], op=ALU.mult),
                     reads=[km[b3], ete[bs]], writes=[kout[bs]])
                yield
                for hd in range(4):
                    k.op('pe', lambda e: e.matmul(psA[:, hd, :], lhsT=kin[bs][:, hd, :], rhs=qin[bs][:, hd, :], start=True, stop=True),
                         reads=[kin[bs], qin[bs]], writes=[psA], partial=(hd > 0))
                yield
                k.op('dve', lambda e: e.tensor_tensor(out=att[bs][:], in0=psA[:],
                                                      in1=tri[:, dd, :].unsqueeze(1).to_broadcast([128, 4, 128]), op=ALU.mult),
                     reads=[psA, tri], writes=[att[bs]])
                if dd == 1:
                    yield
                    self.sigmoid_act(sgt[b3][:], gg[b3][:], [gg[b3]], [sgt[b3]])
                    yield
                    k.op('pool', lambda e: e.tensor_mul(sgt[b3][:], sgt[b3][:], gg[b3][:]), reads=[sgt[b3], gg[b3]], writes=[sgt[b3]])
            def stage2(t, dd=dd, last=last):
                b = t % 2
                b3 = t % 4
                bs = t % 3
                yield
                for hd in range(4):
                    k.op('pe', lambda e: e.matmul(psO[:, hd, :], lhsT=att[bs][:, hd, :], rhs=vt[b3][:, hd * 128:(hd + 1) * 128],
                                                  start=(hd == 0), stop=False), reads=[att[bs], vt[b3]], writes=[psO], partial=(hd > 0))
                    k.op('pe', lambda e: e.matmul(psO[:, hd, :], lhsT=qin[bs][:, hd, :], rhs=Sb[:, hd, :], start=False, stop=True),
                         reads=[qin[bs], Sb], writes=[psO], partial=True)
                yield
                for hd in range(4):
                    k.op('pe', lambda e: e.matmul(psS[:, hd, :], lhsT=kout[bs][:, hd * 64:(hd + 1) * 64],
                                                  rhs=vt[b3][:, hd * 128:(hd + 1) * 128], start=True, stop=True),
                         reads=[kout[bs], vt[b3]], writes=[psS], partial=(hd > 0))
                yield
                for hd in range(4):
                    k.op('dve', lambda e: e.scalar_tensor_tensor(out=S[:, hd, :], in0=S[:, hd, :], scalar=ebT[bs][:, hd, last:last + 1],
                                                                  in1=psS[:, hd, :], op0=ALU.mult, op1=ALU.add),
                         reads=[S, ebT[bs], psS], writes=[S], partial=(hd > 0))
                r = slice(t * 128, (t + 1) * 128)
                o_ = osb[b]
                if dd == 0:
                    k.op('act', lambda e: e.copy(out=o_[:], in_=psO[:]), reads=[psO], writes=[o_])
                    k.op('act', lambda e: e.copy(out=Sb[:], in_=S[:]), reads=[S], writes=[Sb])
                    k.dma('pool', sc["OF"][r, :], o_[:].rearrange("p h d -> p (h d)"), reads=[o_])
                else:
                    k.op('dve', lambda e: e.tensor_tensor(out=o_[:], in0=psO[:], in1=of[b3][:].rearrange("p (h d) -> p h d", h=4),
                                                          op=ALU.add), reads=[psO, of[b3]], writes=[o_])
                    k.op('act', lambda e: e.copy(out=Sb[:], in_=S[:]), reads=[S], writes=[Sb])
            def stage3(t, dd=dd, last=last):
                b = t % 2
                b3 = t % 4
                bs = t % 3
                o_ = osb[b]
                r = slice(t * 128, (t + 1) * 128)
                mm = m2[b]
                for hd in range(4):
                    k.op('act', lambda e: e.activation(out=junk[:], in_=o_[:, hd, :], func=AF.Square, accum_out=mm[:, hd:hd + 1]),
                         reads=[o_], writes=[junk, mm], partial=(hd > 0))
                yield
                self.rstd_col(mm[:, 8:12], mm[:, 0:4], 128, mm[:, 4:8], [mm], [mm])
                yield
                k.op('dve', lambda e: e.tensor_tensor(out=o_[:], in0=o_[:], in1=mm[:, 8:12].unsqueeze(2).to_broadcast([128, 4, 128]),
                                                      op=ALU.mult), reads=[o_, mm], writes=[o_])
                yield
                k.op('pool', lambda e: e.tensor_tensor(out=o_[:], in0=o_[:], in1=gnw[:].unsqueeze(1).to_broadcast([128, 4, 128]),
                                                       op=ALU.mult), reads=[o_, gnw], writes=[o_])
                yield
                k.op('dve', lambda e: e.tensor_tensor(out=ob[b][:], in0=o_[:].rearrange("p h d -> p (h d)"), in1=sgt[b3][:], op=ALU.mult),
                     reads=[o_, sgt[b3]], writes=[ob[b]])
                yield
                k.dma('pool', sc["MIX"][r, 0:512], ob[b][:], reads=[ob[b]])
            loads(order[0], dd)
            loads(order[1], dd)
            loads(order[2], dd)
            for _ in stage1(order[0]):
                pass
            for _ in stage1(order[1]):
                pass
            for oi, t in enumerate(order):
                gens = [stage2(t)]
                if oi + 2 < NT:
                    gens.append(stage1(order[oi + 2]))
                if dd == 1 and oi >= 1:
                    gens.append(stage3(order[oi - 1]))
                self.merge(gens)
                if s1g is not None:
                    for _ in range(6):
                        next(s1g, None)
                if oi + 3 < NT:
                    loads(order[oi + 3], dd)
            if dd == 1:
                for _ in stage3(order[NT - 1]):
                    pass
            k.barrier()
        if s1g is not None:
            for _ in s1g:
                pass
        k.pop()

    def phase_D(self, l):
        k = self.k
        sc = self.sc
        H = sc["H"]
        k.push()
        self.load_consts()
        identb = self.identb
        TB = 8
        mT = k.tile([128, 16, TB * 128], BF16, "mT")
        mb = [k.tile([128, D], BF16, "mx") for _ in range(2)]
        hb = [k.tile([128, D], F32, "hD") for _ in range(TB)]
        pst = [k.psum([128, 8, 128], BF16, "pstD") for _ in range(2)]
        psm = [k.psum([128, 512], F32, "psmD") for _ in range(4)]
        wt = [k.tile([128, 16, 512], BF16, "wD") for _ in range(2)]
        W = self.sc["WOB"][l].rearrange("(c p) n -> p c n", p=128)
        blocks = [(0, 1)] + [(1 + 8 * i, 8) for i in range(4)]
        cn = dict(m=0, p=0, q=0)
        for (t0, nt) in blocks:
            for ti in range(nt):
                t = t0 + ti
                mx = mb[cn['m'] % 2]
                cn['m'] += 1
                k.dma('sp', mx[:], sc["MIX"][t * 128:(t + 1) * 128, :], writes=[mx])
                k.dma('sp', hb[ti][:], H[t * 128:(t + 1) * 128, :], writes=[hb[ti]])
                for half in range(2):
                    p = pst[cn['p'] % 2]
                    cn['p'] += 1
                    for c in range(8):
                        cc = half * 8 + c
                        k.op('pe', lambda e: e.transpose(p[:, c, :], mx[:, cc * 128:(cc + 1) * 128], identb[:]),
                             reads=[mx, identb], writes=[p], partial=(c > 0))
                    if half == 0:
                        k.op('act', lambda e: e.copy(out=mT[:, 0:8, ti * 128:(ti + 1) * 128], in_=p[:]),
                             reads=[p], writes=[mT], partial=True)
                    else:
                        k.op('dve', lambda e: e.tensor_copy(out=mT[:, 8:16, ti * 128:(ti + 1) * 128], in_=p[:]),
                             reads=[p], writes=[mT], partial=True)
            k.dma('pool', wt[0][:], W[:, :, 0:512], writes=[wt[0]])
            for j in range(4):
                if j + 1 < 4:
                    k.dma('pool', wt[(j + 1) % 2][:], W[:, :, (j + 1) * 512:(j + 2) * 512], writes=[wt[(j + 1) % 2]])
                w = wt[j % 2]
                for ti in range(nt):
                    p = psm[cn['q'] % 4]
                    cn['q'] += 1
                    for c in range(16):
                        k.op('pe', lambda e: e.matmul(p[:], lhsT=mT[:, c, ti * 128:(ti + 1) * 128], rhs=w[:, c, :],
                                                      start=(c == 0), stop=(c == 15)), reads=[w, mT], writes=[p], partial=(c > 0))
                    k.op('dve', lambda e: e.tensor_tensor(out=hb[ti][:, j * 512:(j + 1) * 512], in0=p[:],
                                                          in1=hb[ti][:, j * 512:(j + 1) * 512], op=ALU.add),
                         reads=[p, hb[ti]], writes=[hb[ti]], partial=True)
            for ti in range(nt):
                t = t0 + ti
                k.dma('pool', H[t * 128:(t + 1) * 128, :], hb[ti][:], reads=[hb[ti]])
        k.pop()

    def phase_F(self):
        k = self.k
        H = self.sc["H"]
        k.push()
        nwb = k.tile([128, D], F32, "fnw")
        k.dma('sp', nwb[:], bc_rows(self.final_norm_w[0:1, :]), writes=[nwb])
        hb = [k.tile([128, D], F32, "hF") for _ in range(3)]
        ob = [k.tile([128, D], F32, "oF") for _ in range(3)]
        junk = k.tile([128, D], BF16, "junkF")
        ss = [k.tile([128, 4], F32, "ssF") for _ in range(3)]
        for t in range(1, NT):
            i = t % 3
            h, o, s2 = hb[i], ob[i], ss[i]
            k.dma('sp', h[:], H[t * 128:(t + 1) * 128, :], writes=[h])
            k.op('act', lambda e: e.activation(out=junk[:], in_=h[:], func=AF.Square, accum_out=s2[:, 0:1]),
                 reads=[h], writes=[junk, s2])
            self.rstd_col(s2[:, 2:3], s2[:, 0:1], D, s2[:, 1:2], [s2], [s2])
            k.op('dve', lambda e: e.scalar_tensor_tensor(out=o[:], in0=h[:], scalar=s2[:, 2:3], in1=nwb[:],
                                                          op0=ALU.mult, op1=ALU.mult), reads=[h, s2, nwb], writes=[o])
            k.dma('pool', self.out[(t - 1) * 128:t * 128, :], o[:], reads=[o])
        k.pop()

    def build(self):
        k = self.k
        if self.want('init'):
            self.phase_init()
        if self.want('prep'):
            self.phase_prep()
        for l in range(self.layers):
            if self.want('A'):
                self.phase_A(l)
            if self.want('B'):
                self.phase_B(l)
            if self.want('C'):
                self.phase_C(l)
            if self.want('S1') and not self.merge_s1:
                self.phase_S1(l)
            if self.want('S2'):
                self.phase_S2(l)
            if self.want('D'):
                self.phase_D(l)
        if self.want('F'):
            self.phase_F()
        k.push()
        k.pop()
        return k.nc


def _t5_bucket(rel):
    nb = 16
    max_exact = 8
    ret = nb if rel > 0 else 0
    n = abs(rel)
    if n < max_exact:
        return ret + n
    nf = np.float32(max(n, 1))
    large = max_exact + int(np.float32(np.log(nf / np.float32(max_exact))) / np.float32(math.log(128 / max_exact))
                            * np.float32(nb - max_exact))
    return ret + min(large, nb - 1)


def host_consts():
    ident = np.eye(128, dtype=np.float32)
    s = np.arange(128)[:, None]
    t = np.arange(128)[None, :]
    tri = np.stack([(s <= t), (s >= t), (s > t), (s < t)]).astype(np.float32)
    onehot = np.zeros((32, 512), np.float32)
    for m_ in range(511):
        onehot[_t5_bucket(255 - m_), m_] = 1.0
    return dict(c_ident=ident, c_tri=tri, c_onehot=onehot)


def make_in_maps(inputs):
    consts = host_consts()
    maps = []
    shared = {}
    for n in ("meta_tokens", "rel_bias", "norm_w", "w_in", "w_out", "gla_wa2", "gla_norm_w",
              "diff_norm_w", "conv_w", "conv_b", "ssd_D", "ssd_norm_w"):
        shared[n] = np.ascontiguousarray(np.asarray(inputs[n], dtype=np.float32))
    shared["final_norm_w"] = np.asarray(inputs["final_norm_w"], np.float32).reshape(1, D)
    shared["gla_ba"] = np.asarray(inputs["gla_ba"], np.float32).reshape(DEPTH, 2, 256)
    shared["diff_lambda"] = np.asarray(inputs["diff_lambda"], np.float32).reshape(DEPTH, 256)
    shared["ssd_A_log"] = np.asarray(inputs["ssd_A_log"], np.float32).reshape(DEPTH, 32)
    shared["ssd_dt_bias"] = np.asarray(inputs["ssd_dt_bias"], np.float32).reshape(DEPTH, 32)
    shared.update(consts)
    x = np.asarray(inputs["x"], np.float32)
    for b in range(8):
        m = dict(shared)
        m["x"] = np.ascontiguousarray(x[b])
        maps.append(m)
    return maps


def kernel(**inputs):
    m = Model()
    nc = m.build()
    maps = make_in_maps(inputs)
    res = run_bass_kernel_spmd(nc, maps, core_ids=list(range(8)))
    out = np.stack([np.asarray(res.results[b]["out"]) for b in range(8)], axis=0)
    return out.astype(np.float32)
```

```python
import contextlib
import math
import numpy as np
import concourse.bass as bass
import concourse.mybir as mybir
from concourse.bass_utils import run_bass_kernel_spmd

F32 = mybir.dt.float32
BF16 = mybir.dt.bfloat16
AF = mybir.ActivationFunctionType
ALU = mybir.AluOpType
AX = mybir.AxisListType

D = 2048
L = 4112
NT = 33
LP = NT * 128
PADR = 112
DEPTH = 4
INTOT = 6208
EPS = 1e-6
NEG = -30000.0

C_GQ, C_GK, C_GV, C_GG, C_GC = 0, 256, 512, 1024, 1536
C_DQ, C_DK, C_DV, C_DG = 1568, 2080, 2592, 3104
C_Z, C_XBC, C_DT = 3616, 4640, 6176


class Tl:
    def __init__(self, name, h):
        self.name = name
        self.h = h
        self.wf = {}
        self.wp = {}
        self.r = {}
        self.dsem = None

    def __getitem__(self, k):
        return self.h[k]


def _mx(d, k, v):
    if d.get(k, 0) < v:
        d[k] = v


class KB:
    ENG = ('pe', 'act', 'dve', 'pool', 'sp')

    def __init__(self):
        self.nc = bass.Bass("TRN2", target_bir_lowering=False)
        nc = self.nc
        self.engs = dict(pe=nc.tensor, act=nc.scalar, dve=nc.vector, pool=nc.gpsimd, sp=nc.sync)
        self.root = contextlib.ExitStack()
        self.sems = {}
        self.cnt = {}
        for e in self.ENG:
            self.sems[e] = self.root.enter_context(nc.semaphore("s_" + e))
            self.cnt[e] = 0
        self.sems['bar'] = self.root.enter_context(nc.semaphore("s_bar"))
        self.cnt['bar'] = 0
        self.ndsem = 84
        self.free_dsem = []
        for i in range(self.ndsem):
            nm = "d%d" % i
            self.sems[nm] = self.root.enter_context(nc.semaphore(nm))
            self.cnt[nm] = 0
            self.free_dsem.append(nm)
        self.waited = {e: {} for e in self.ENG}
        self.scopes = []
        self.uid = 0
        self.ninstr = 0

    def push(self):
        self.scopes.append((contextlib.ExitStack(), []))

    def pop(self):
        self.barrier()
        es, tiles = self.scopes.pop()
        for t in tiles:
            if t.dsem is not None:
                self.free_dsem.append(t.dsem)
                t.dsem = None
        es.close()

    def tile(self, shape, dtype, name=None, space='sbuf'):
        self.uid += 1
        nm = "%s_%d" % (name or 't', self.uid)
        es, tiles = self.scopes[-1]
        if space == 'sbuf':
            h = es.enter_context(self.nc.sbuf_tensor(nm, list(shape), dtype))
        else:
            h = es.enter_context(self.nc.psum_tensor(nm, list(shape), dtype))
        t = Tl(nm, h)
        tiles.append(t)
        return t

    def psum(self, shape, dtype=F32, name=None):
        return self.tile(shape, dtype, name or 'ps', space='psum')

    def dram(self, name, shape, dtype, kind="Internal"):
        return self.nc.dram_tensor(name, list(shape), dtype, kind=kind).ap()

    def _wait(self, eng, need):
        w = self.waited[eng]
        for s, v in need.items():
            if s == 'pe' and eng == 'pe':
                continue
            if w.get(s, 0) >= v:
                continue
            self.engs[eng].wait_ge(self.sems[s], v)
            self.ninstr += 1
            w[s] = v

    def _deps(self, reads, writes, partial):
        need = {}
        for t in reads:
            for d in (t.wf, t.wp):
                for s, v in d.items():
                    _mx(need, s, v)
        for t in writes:
            for s, v in t.r.items():
                _mx(need, s, v)
            for s, v in t.wf.items():
                _mx(need, s, v)
            if not partial:
                for s, v in t.wp.items():
                    _mx(need, s, v)
        return need

    def _reg(self, s, v, reads, writes, partial):
        for t in reads:
            _mx(t.r, s, v)
        for t in writes:
            if partial:
                _mx(t.wp, s, v)
            else:
                t.wf = {s: v}
                t.wp = {}
                t.r = {}

    def op(self, eng, fn, reads=(), writes=(), partial=False):
        self._wait(eng, self._deps(reads, writes, partial))
        ins = fn(self.engs[eng])
        self.cnt[eng] += 1
        ins.then_inc(self.sems[eng], 1)
        self.ninstr += 1
        self._reg(eng, self.cnt[eng], reads, writes, partial)
        return ins

    def dma(self, q, out, in_, reads=(), writes=(), partial=False, **kw):
        self._wait(q, self._deps(reads, writes, partial))
        owner = (list(writes) + list(reads))[0]
        if owner.dsem is None:
            owner.dsem = self.free_dsem.pop()
        s = owner.dsem
        ins = self.engs[q].dma_start(out=out, in_=in_, **kw)
        self.cnt[s] += 16
        ins.then_inc(self.sems[s], 16)
        self.ninstr += 1
        self._reg(s, self.cnt[s], reads, writes, partial)
        return ins

    def dma_dd(self, q, out, in_, **kw):
        if not hasattr(self, '_ddt'):
            self._ddt = Tl('dd', None)
            self._ddt.dsem = self.free_dsem.pop()
        s = self._ddt.dsem
        ins = self.engs[q].dma_start(out=out, in_=in_, **kw)
        self.cnt[s] += 16
        ins.then_inc(self.sems[s], 16)
        self.ninstr += 1

    def barrier(self):
        sp = self.engs['sp']
        for s, v in self.cnt.items():
            if s in ('bar', 'sp') or v == 0:
                continue
            if self.waited['sp'].get(s, 0) >= v:
                continue
            sp.wait_ge(self.sems[s], v)
            self.waited['sp'][s] = v
            self.ninstr += 1
        self.cnt['bar'] += 1
        sp.sem_inc(self.sems['bar'], 1)
        for e in self.ENG:
            if e == 'sp':
                continue
            self.engs[e].wait_ge(self.sems['bar'], self.cnt['bar'])
            for s, v in self.cnt.items():
                self.waited[e][s] = v
        for s, v in self.cnt.items():
            self.waited['sp'][s] = v
        for es, tiles in self.scopes:
            for t in tiles:
                t.wf, t.wp, t.r = {}, {}, {}


def bc_rows(ap_row, n=128):
    return ap_row.to_broadcast([n, ap_row.shape[-1]])


class Model:
    def __init__(self, debug_out=(), layers=DEPTH, phases=None):
        self.k = KB()
        self.debug_out = set(debug_out)
        self.layers = layers
        self.phases = phases
        self.merge_s1 = False
        k = self.k
        ei = lambda n, s: k.dram(n, s, F32, kind="ExternalInput")
        self.x = ei("x", [4096, D])
        self.meta = ei("meta_tokens", [16, D])
        self.rel_bias = ei("rel_bias", [32, 4])
        self.final_norm_w = ei("final_norm_w", [1, D])
        self.norm_w = ei("norm_w", [DEPTH, D])
        self.w_in = ei("w_in", [DEPTH, D, INTOT])
        self.w_out = ei("w_out", [DEPTH, D, D])
        self.gla_wa2 = ei("gla_wa2", [DEPTH, 2, 16, 256])
        self.gla_ba = ei("gla_ba", [DEPTH, 2, 256])
        self.gla_norm_w = ei("gla_norm_w", [DEPTH, 128])
        self.diff_lambda = ei("diff_lambda", [DEPTH, 256])
        self.diff_norm_w = ei("diff_norm_w", [DEPTH, 128])
        self.conv_w = ei("conv_w", [DEPTH, 5, 1536])
        self.conv_b = ei("conv_b", [DEPTH, 1536])
        self.ssd_A_log = ei("ssd_A_log", [DEPTH, 32])
        self.ssd_dt_bias = ei("ssd_dt_bias", [DEPTH, 32])
        self.ssd_D = ei("ssd_D", [DEPTH, 16])
        self.ssd_norm_w = ei("ssd_norm_w", [DEPTH, 1024])
        self.c_ident = ei("c_ident", [128, 128])
        self.c_tri = ei("c_tri", [4, 128, 128])
        self.c_onehot = ei("c_onehot", [32, 3 * 128 + 128])
        self.out = k.dram("out", [4096, D], F32, kind="ExternalOutput")
        self.sc = {}
        sc = self.scr
        sc("H", [LP, D], F32)
        sc("GQT", [256, LP], BF16)
        sc("GKT", [256, LP], BF16)
        sc("GCT", [32, LP], F32)
        sc("DQT", [512, LP], BF16)
        sc("DKT", [512, LP], BF16)
        sc("XBCT", [1536, LP], F32)
        sc("GK", [LP, 256], BF16)
        sc("GV", [LP, 512], BF16)
        sc("GG", [LP, 512], F32)
        sc("DV", [LP, 512], BF16)
        sc("DG", [LP, 512], F32)
        sc("Z", [LP, 1024], F32)
        sc("DT", [LP, 32], F32)
        sc("MIX", [LP, D], BF16)
        sc("XS", [LP, 1024], F32)
        sc("BM", [LP, 256], BF16)
        sc("BT", [256, LP], BF16)
        sc("CT", [256, LP], BF16)
        sc("YF", [LP, 1024], F32)
        sc("OF", [LP, 512], F32)
        sc("WB", [DEPTH, D, INTOT], BF16)
        sc("WOB", [DEPTH, D, D], BF16)

    def scr(self, name, shape, dtype):
        kind = "ExternalOutput" if name in self.debug_out else "Internal"
        self.sc[name] = self.k.dram("sc_" + name, shape, dtype, kind=kind)
        return self.sc[name]

    def want(self, ph):
        return self.phases is None or ph in self.phases

    def phase_init(self):
        k = self.k
        H = self.sc["H"]
        k.push()
        z = k.tile([PADR, D], F32, "zero")
        k.op('dve', lambda e: e.memset(z[:], 0.0), writes=[z])
        k.dma('sp', H[0:PADR, :], z[:], reads=[z])
        k.dma_dd('sp', H[PADR:128, :], self.meta[:, :])
        for i in range(8):
            k.dma_dd('sp' if i % 2 == 0 else 'act', H[128 + i * 512:128 + (i + 1) * 512, :],
                     self.x[i * 512:(i + 1) * 512, :])
        k.pop()

    def load_consts(self):
        k = self.k
        self.identb = k.tile([128, 128], BF16, "identb")
        k.dma('pool', self.identb[:], self.c_ident[:, :], writes=[self.identb])


    def wconv_steps(self, l, bufs):
        k = self.k
        WB, WOB = self.sc["WB"], self.sc["WOB"]
        blocks = []
        c0 = 0
        while c0 < INTOT:
            n = min(512, INTOT - c0)
            blocks.append((self.w_in[l].rearrange("(c p) n -> p c n", p=128)[:, :, c0:c0 + n],
                           WB[l].rearrange("(c p) n -> p c n", p=128)[:, :, c0:c0 + n], n))
            c0 += n
        for j in range(4):
            blocks.append((self.w_out[l].rearrange("(c p) n -> p c n", p=128)[:, :, j * 512:(j + 1) * 512],
                           WOB[l].rearrange("(c p) n -> p c n", p=128)[:, :, j * 512:(j + 1) * 512], 512))
        steps = []
        nb = len(blocks)
        for i in range(nb + 1):
            def f(i=i):
                if i < nb:
                    src, dst, n = blocks[i]
                    w = bufs[i % 2]
                    k.dma('pool', w[:, :, 0:n], src, writes=[w])
                if i >= 1:
                    src, dst, n = blocks[i - 1]
                    w = bufs[(i - 1) % 2]
                    k.dma('sp', dst, w[:, :, 0:n], reads=[w])
            steps.append(f)
        return steps

    def norm_transpose_block(self, src_tiles_loader, nt, uT, nwb, zero_pad_first):
        pass

    def phase_A(self, l):
        k = self.k
        H = self.sc["H"]
        k.push()
        self.load_consts()
        identb = self.identb
        nwb = k.tile([128, D], F32, "nwb")
        k.dma('sp', nwb[:], bc_rows(self.norm_w[l:l + 1, :]), writes=[nwb])
        TB = 8
        uTs = [k.tile([128, 16, TB * 128], BF16, "uT") for _ in range(2)]
        hb = [k.tile([128, D], F32, "h") for _ in range(2)]
        ub = [k.tile([128, D], BF16, "u") for _ in range(2)]
        junk = k.tile([128, D], BF16, "junk")
        ss = [k.tile([128, 2], F32, "ss") for _ in range(2)]
        pst = [k.psum([128, 8, 128], BF16, "pst") for _ in range(2)]
        psm = [k.psum([128, 512], F32, "psm") for _ in range(4)]
        wfm = [k.tile([128, 16, 128], BF16, "wfm") for _ in range(3)]
        wtm = [k.tile([128, 16, 512], BF16, "wtm") for _ in range(2)]
        ofm = [k.tile([128, 1024], F32, "ofm") for _ in range(2)]
        ofmb = [k.tile([128, 1024], BF16, "ofmb") for _ in range(2)]
        otm = [k.tile([128, 512], F32, "otm") for _ in range(2)]
        otmb = [k.tile([128, 512], BF16, "otmb") for _ in range(2)]
        W = self.sc["WB"][l].rearrange("(c p) n -> p c n", p=128)
        sc = self.sc
        fm = []
        for j in range(2):
            fm.append((C_GQ + 128 * j, 128, sc["GQT"], 128 * j, BF16, 0.125))
        for j in range(2):
            fm.append((C_GK + 128 * j, 128, sc["GKT"], 128 * j, BF16, 1.0))
        fm.append((C_GC, 32, sc["GCT"], 0, F32, 1.0))
        for j in range(4):
            fm.append((C_DQ + 128 * j, 128, sc["DQT"], 128 * j, BF16, 0.125))
        for j in range(4):
            fm.append((C_DK + 128 * j, 128, sc["DKT"], 128 * j, BF16, 1.0))
        for j in range(12):
            fm.append((C_XBC + 128 * j, 128, sc["XBCT"], 128 * j, F32, 1.0))
        tm = [(C_GK, 256, sc["GK"], 0, BF16), (C_GV, 512, sc["GV"], 0, BF16), (C_GG, 512, sc["GG"], 0, F32),
              (C_DV, 512, sc["DV"], 0, BF16), (C_DG, 512, sc["DG"], 0, F32), (C_Z, 512, sc["Z"], 0, F32),
              (C_Z + 512, 512, sc["Z"], 512, F32), (C_DT, 32, sc["DT"], 0, F32)]
        blocks = [(0, 1)] + [(1 + 8 * i, 8) for i in range(4)]
        cntr = dict(h=0, pst=0, fm=0, tm=0, psm=0, ofm=0, otm=0)
        def norm_block(bi):
            t0, nt = blocks[bi]
            uT = uTs[bi % 2]
            for ti in range(nt):
                t = t0 + ti
                i = cntr['h'] % 2
                cntr['h'] += 1
                h, u, s2 = hb[i], ub[i], ss[i]
                k.dma('sp', h[:], H[t * 128:(t + 1) * 128, :], writes=[h])
                k.op('act', lambda e: e.activation(out=junk[:], in_=h[:], func=AF.Square, accum_out=s2[:, 0:1]),
                     reads=[h], writes=[junk, s2])
                k.op('dve', lambda e: e.tensor_scalar(s2[:, 1:2], s2[:, 0:1], 1.0 / D, EPS, ALU.mult, ALU.add),
                     reads=[s2], writes=[s2])
                k.op('act', lambda e: e.activation(out=s2[:, 1:2], in_=s2[:, 1:2], func=AF.Ln), reads=[s2], writes=[s2])
                k.op('act', lambda e: e.activation(out=s2[:, 0:1], in_=s2[:, 1:2], func=AF.Exp, scale=-0.5),
                     reads=[s2], writes=[s2])
                k.op('dve', lambda e: e.scalar_tensor_tensor(out=u[:], in0=h[:], scalar=s2[:, 0:1], in1=nwb[:],
                                                              op0=ALU.mult, op1=ALU.mult),
                     reads=[h, s2, nwb], writes=[u])
                if t == 0:
                    k.op('dve', lambda e: e.memset(u[0:PADR, :], 0.0), writes=[u])
                for half in range(2):
                    p = pst[cntr['pst'] % 2]
                    cntr['pst'] += 1
                    for c in range(8):
                        cc = half * 8 + c
                        k.op('pe', lambda e: e.transpose(p[:, c, :], u[:, cc * 128:(cc + 1) * 128], identb[:]),
                             reads=[u, identb], writes=[p], partial=(c > 0))
                    k.op('act' if half == 0 else 'dve',
                         (lambda e: e.copy(out=uT[:, half * 8:half * 8 + 8, ti * 128:(ti + 1) * 128], in_=p[:]))
                         if half == 0 else
                         (lambda e: e.tensor_copy(out=uT[:, half * 8:half * 8 + 8, ti * 128:(ti + 1) * 128], in_=p[:])),
                         reads=[p], writes=[uT], partial=True)
        norm_block(0)
        for bi, (t0, nt) in enumerate(blocks):
            ntok = nt * 128
            uT = uTs[bi % 2]
            def ld_fm(j):
                col0, nr = fm[j][0], fm[j][1]
                w = wfm[j % 3]
                k.dma('pool', w[:, :, 0:nr], W[:, :, col0:col0 + nr], writes=[w])
            if bi == 0:
                ld_fm(0)
                ld_fm(1)
            for j in range(len(fm)):
                if j + 2 < len(fm):
                    ld_fm(j + 2)
                col0, nr, dst, r0, dt, scale = fm[j]
                w = wfm[j % 3]
                ii = cntr['ofm'] % 2
                cntr['ofm'] += 1
                o = ofm[ii] if dt == F32 else ofmb[ii]
                for n0 in range(0, ntok, 512):
                    nn = min(512, ntok - n0)
                    p = psm[cntr['psm'] % 4]
                    cntr['psm'] += 1
                    for c in range(16):
                        k.op('pe', lambda e: e.matmul(p[0:nr, 0:nn], lhsT=w[:, c, 0:nr], rhs=uT[:, c, n0:n0 + nn],
                                                      start=(c == 0), stop=(c == 15)),
                             reads=[w, uT], writes=[p], partial=(c > 0))
                    k.op('act', lambda e: e.activation(out=o[0:nr, n0:n0 + nn], in_=p[0:nr, 0:nn], func=AF.Copy, scale=scale),
                         reads=[p], writes=[o], partial=(n0 > 0))
                k.dma('sp', dst[r0:r0 + nr, t0 * 128:t0 * 128 + ntok], o[0:nr, 0:ntok], reads=[o])
            if bi + 1 < len(blocks):
                norm_block(bi + 1)
            def ld_tm(j):
                col0, ncl = tm[j][0], tm[j][1]
                w = wtm[j % 2]
                k.dma('pool', w[:, :, 0:ncl], W[:, :, col0:col0 + ncl], writes=[w])
            ld_tm(0)
            for j in range(len(tm)):
                if j + 1 < len(tm):
                    ld_tm(j + 1)
                col0, ncl, dst, c0, dt = tm[j]
                w = wtm[j % 2]
                for ti in range(nt):
                    t = t0 + ti
                    p = psm[cntr['psm'] % 4]
                    cntr['psm'] += 1
                    for c in range(16):
                        k.op('pe', lambda e: e.matmul(p[:, 0:ncl], lhsT=uT[:, c, ti * 128:(ti + 1) * 128],
                                                      rhs=w[:, c, 0:ncl], start=(c == 0), stop=(c == 15)),
                             reads=[w, uT], writes=[p], partial=(c > 0))
                    ii = cntr['otm'] % 2
                    cntr['otm'] += 1
                    o = otm[ii] if dt == F32 else otmb[ii]
                    k.op('dve', lambda e: e.tensor_copy(out=o[:, 0:ncl], in_=p[:, 0:ncl]), reads=[p], writes=[o])
                    k.dma('sp', dst[t * 128:(t + 1) * 128, c0:c0 + ncl], o[:, 0:ncl], reads=[o])
                if j == len(tm) - 3 and bi + 1 < len(blocks):
                    ld_fm(0)
                    ld_fm(1)
        k.pop()


    def phase_prep(self):
        k = self.k
        self.scr("RB", [4, 128, 512], F32)
        RB = self.sc["RB"]
        k.push()
        rb = k.tile([32, 4], F32, "rb")
        oh = k.tile([32, 512], F32, "oh")
        k.dma('sp', rb[:], self.rel_bias[:, :], writes=[rb])
        k.dma('sp', oh[:], self.c_onehot[:, :], writes=[oh])
        for h in range(4):
            p = k.psum([128, 512], F32, "prb")
            o = k.tile([128, 512], F32, "orb")
            k.op('pe', lambda e: e.matmul(p[:], lhsT=rb[:, h:h + 1].to_broadcast([32, 128]), rhs=oh[:],
                                          start=True, stop=True), reads=[rb, oh], writes=[p])
            k.op('dve', lambda e: e.tensor_copy(out=o[:], in_=p[:]), reads=[p], writes=[o])
            k.dma('sp', RB[h], o[:], reads=[o])
        wb = [k.tile([128, 16, 512], BF16, "wcv") for _ in range(2)]
        for f in self.wconv_steps(0, wb):
            f()
        k.pop()

    def merge(self, gens):
        gens = list(gens)
        while gens:
            for g in list(gens):
                try:
                    next(g)
                except StopIteration:
                    gens.remove(g)

    def rstd_col(self, out_col, ss_col, n, tmp_col, reads, writes):
        k = self.k
        k.op('dve', lambda e: e.tensor_scalar(tmp_col, ss_col, 1.0 / n, EPS, ALU.mult, ALU.add), reads=reads, writes=writes)
        k.op('act', lambda e: e.activation(out=tmp_col, in_=tmp_col, func=AF.Ln), reads=writes, writes=writes)
        k.op('act', lambda e: e.activation(out=out_col, in_=tmp_col, func=AF.Exp, scale=-0.5), reads=writes, writes=writes)

    def ones_col(self):
        k = self.k
        self.onesc = k.tile([128, 1], F32, "onesc")
        k.op('dve', lambda e: e.memset(self.onesc[:], 1.0), writes=[self.onesc])
        return self.onesc

    def sigmoid_act(self, out, in_, reads, writes):
        k = self.k
        oc = self.onesc
        npart = out.shape[0]
        k.op('act', lambda e: e.activation(out=out, in_=in_, func=AF.Exp, scale=-1.0), reads=reads, writes=writes)
        k.op('act', lambda e: e.activation(out=out, in_=out, func=AF.Ln, bias=oc[0:npart, 0:1], scale=1.0),
             reads=list(writes) + [oc], writes=writes)
        k.op('act', lambda e: e.activation(out=out, in_=out, func=AF.Exp, scale=-1.0), reads=writes, writes=writes)

    def silu_gate(self, out, y, g, tmp, n, reads, writes, eng2='dve'):
        k = self.k
        self.sigmoid_act(tmp, g, reads, writes)
        k.op(eng2, lambda e: e.tensor_mul(tmp, tmp, g), reads=list(reads) + list(writes), writes=writes)
        k.op('dve', lambda e: e.tensor_mul(out, tmp, y), reads=list(reads) + list(writes), writes=writes)

    def phase_C(self, l):
        k = self.k
        sc = self.sc
        lambda_init = 0.8 - 0.6 * math.exp(-0.3 * l)
        k.push()
        self.ones_col()
        RB = sc["RB"]
        identf = k.tile([128, 128], F32, "identf")
        k.dma('sp', identf[:], self.c_ident[:, :], writes=[identf])
        biasT = k.tile([128, 4, 3, 128], F32, "biasT")
        for h in range(4):
            for di, d in enumerate((-1, 0, 1)):
                off = 255 - 128 * d
                src = bass.AP(tensor=RB.tensor, offset=h * 128 * 512 + off, ap=[[511, 128], [1, 128]])
                k.dma('sp', biasT[:, h, di, :], src, writes=[biasT], partial=True)
        cb = k.tile([128, 16], F32, "cb")
        k.op('dve', lambda e: e.memset(cb[:], 0.0), writes=[cb])
        k.op('dve', lambda e: e.memset(cb[0:PADR, 13:14], NEG), writes=[cb])
        k.dma('sp', cb[:, 0:4], bc_rows(self.rel_bias[15:16, :]), writes=[cb], partial=True)
        k.dma('sp', cb[:, 4:8], bc_rows(self.rel_bias[31:32, :]), writes=[cb], partial=True)
        k.op('dve', lambda e: e.tensor_tensor(out=cb[:, 8:12], in0=cb[:, 0:4], in1=cb[:, 13:14].to_broadcast([128, 4]),
                                              op=ALU.add), reads=[cb], writes=[cb])
        lp = k.tile([128, 256], F32, "lp")
        k.dma('sp', lp[:], bc_rows(self.diff_lambda[l:l + 1, :]), writes=[lp])
        lw = k.tile([128, 8], F32, "lw")
        lj = k.tile([128, 64], F32, "lj")
        for i in range(2):
            k.op('dve', lambda e: e.tensor_tensor(out=lj[:], in0=lp[:, 128 * i:128 * i + 64],
                                                  in1=lp[:, 128 * i + 64:128 * i + 128], op=ALU.mult),
                 reads=[lp], writes=[lj])
            k.op('dve', lambda e: e.reduce_sum(out=lw[:, i:i + 1], in_=lj[:], axis=AX.X), reads=[lj], writes=[lw])
        k.op('act', lambda e: e.activation(out=lw[:, 2:4], in_=lw[:, 0:2], func=AF.Exp), reads=[lw], writes=[lw])
        k.op('dve', lambda e: e.tensor_tensor(out=lw[:, 4:5], in0=lw[:, 3:4], in1=lw[:, 2:3], op=ALU.subtract),
             reads=[lw], writes=[lw])
        k.op('dve', lambda e: e.tensor_scalar_add(lw[:, 5:6], lw[:, 4:5], -lambda_init), reads=[lw], writes=[lw])
        nlam = lw[:, 5:6]
        snw = k.tile([128, 128], F32, "snw")
        k.dma('sp', snw[:], bc_rows(self.diff_norm_w[l:l + 1, :]), writes=[snw])
        k.op('dve', lambda e: e.tensor_scalar_mul(snw[:], snw[:], 1.0 - lambda_init), reads=[snw], writes=[snw])

        kT = [k.tile([128, LP], BF16, "kT") for _ in range(2)]
        qT = [k.tile([128, LP], BF16, "qT") for _ in range(2)]
        va = [k.tile([128, NT, 129], BF16, "va") for _ in range(2)]
        pss = [[k.psum([128, 384], F32, "pss") for _ in range(2)] for _ in range(2)]
        acc = [[k.psum([128, 3, 129], F32, "acc") for _ in range(2)] for _ in range(2)]
        pT = [[k.tile([128, 384], BF16, "pT") for _ in range(2)] for _ in range(2)]
        dg = [k.tile([128, 3, 128], F32, "dg") for _ in range(2)]
        sg = [k.tile([128, 3, 128], F32, "sg") for _ in range(2)]
        o1 = [k.tile([128, 3, 128], F32, "o1") for _ in range(2)]
        ot = [k.tile([128, 3, 128], F32, "ot") for _ in range(2)]
        sq = [k.tile([128, 3, 128], F32, "sq") for _ in range(2)]
        ob = [k.tile([128, 3, 128], BF16, "ob") for _ in range(2)]
        sm = [k.tile([128, 8, 3], F32, "sm") for _ in range(2)]

        def load_head(h):
            b = h % 2
            k.dma('sp', kT[b][:], sc["DKT"][h * 128:(h + 1) * 128, :], writes=[kT[b]])
            k.dma('sp', qT[b][:], sc["DQT"][h * 128:(h + 1) * 128, :], writes=[qT[b]])
            k.dma('sp', va[b][:, :, 0:128],
                  sc["DV"][:, h * 128:(h + 1) * 128].rearrange("(t p) c -> p t c", p=128), writes=[va[b]])
            k.op('pool', lambda e: e.memset(va[b][:, :, 128:129], 1.0), writes=[va[b]], partial=True)

        load_head(0)
        ecnt = [0]
        steps = [(h, qb, j) for h in range(4) for qb in range(11) for j in range(NT)]

        def runs_of(qb, j):
            i0 = 3 * qb
            cls = []
            for s_ in range(3):
                d = j - (i0 + s_)
                if d < -1:
                    cls.append(('lo', None))
                elif d > 1:
                    cls.append(('hi', None))
                else:
                    cls.append(('near', d))
            runs = []
            for s_ in range(3):
                if cls[s_][0] != 'near' and runs and runs[-1][2] == cls[s_]:
                    runs[-1][1] = s_ + 1
                else:
                    runs.append([s_, s_ + 1, cls[s_]])
            return runs

        def scores(si):
            h, qb, j = steps[si]
            hb, sb, i0 = h % 2, si % 2, 3 * qb
            for c in range(2):
                p = pss[sb][c]
                first = True
                for (s0, s1, cl) in runs_of(qb, j):
                    near = cl[0] == 'near'
                    k.op('pe', lambda e: e.matmul(p[:, s0 * 128:s1 * 128],
                                                  lhsT=kT[hb][c * 64:(c + 1) * 64, j * 128:(j + 1) * 128],
                                                  rhs=qT[hb][c * 64:(c + 1) * 64, (i0 + s0) * 128:(i0 + s1) * 128],
                                                  start=True, stop=not near),
                         reads=[kT[hb], qT[hb]], writes=[p], partial=not first)
                    first = False
                    if near:
                        k.op('pe', lambda e: e.matmul(p[:, s0 * 128:s1 * 128], lhsT=identf[:],
                                                      rhs=biasT[:, h, cl[1] + 1, :], start=False, stop=True),
                             reads=[identf, biasT], writes=[p], partial=True)

        def exps(si):
            h, qb, j = steps[si]
            sb = si % 2
            for c in range(2):
                p = pss[sb][c]
                first = True
                for (s0, s1, cl) in runs_of(qb, j):
                    if cl[0] == 'near':
                        col = 13 if j == 0 else 12
                    elif cl[0] == 'lo':
                        col = (8 + h) if j == 0 else h
                    else:
                        col = 4 + h
                    k.op('act', lambda e: e.activation(out=pT[sb][c][:, s0 * 128:s1 * 128],
                                                       in_=p[:, s0 * 128:s1 * 128], func=AF.Exp,
                                                       bias=cb[:, col:col + 1], scale=1.0),
                         reads=[p, cb], writes=[pT[sb][c]], partial=not first)
                    first = False

        def pv(si):
            h, qb, j = steps[si]
            hb, sb, ab = h % 2, si % 2, (h * 11 + qb) % 2
            for c in range(2):
                for s_ in range(3):
                    k.op('pe', lambda e: e.matmul(acc[ab][c][:, s_, :], lhsT=pT[sb][c][:, s_ * 128:(s_ + 1) * 128],
                                                  rhs=va[hb][:, j, :], start=(j == 0 and s_ == 0), stop=(j == NT - 1)),
                         reads=[pT[sb][c], va[hb]], writes=[acc[ab][c]], partial=not (j == 0 and s_ == 0))

        def rows3(dram, qb, c0):
            i0 = 3 * qb
            return dram[i0 * 128:(i0 + 3) * 128, c0:c0 + 128].rearrange("(a p) c -> p a c", p=128)

        def gate_pre(h, qb):
            eb = (h * 11 + qb) % 2
            k.dma('sp', dg[eb][:], rows3(sc["DG"], qb, h * 128), writes=[dg[eb]])
            self.sigmoid_act(sg[eb][:], dg[eb][:], [dg[eb]], [sg[eb]])
            k.op('pool', lambda e: e.tensor_mul(sg[eb][:], sg[eb][:], dg[eb][:]), reads=[sg[eb], dg[eb]], writes=[sg[eb]])

        def epi1(h, qb):
            ab = (h * 11 + qb) % 2
            eb = ab
            a0, a1 = acc[ab][0], acc[ab][1]
            m = sm[eb]
            k.op('dve', lambda e: e.reciprocal(m[:, 0, :], a0[:, :, 128]), reads=[a0], writes=[m])
            k.op('dve', lambda e: e.reciprocal(m[:, 1, :], a1[:, :, 128]), reads=[a1], writes=[m])
            k.op('dve', lambda e: e.tensor_tensor(out=m[:, 2, :], in0=m[:, 1, :], in1=nlam.to_broadcast([128, 3]), op=ALU.mult),
                 reads=[m, lw], writes=[m])
            k.op('dve', lambda e: e.tensor_tensor(out=o1[eb][:], in0=a0[:, :, 0:128],
                                                  in1=m[:, 0, :].unsqueeze(2).to_broadcast([128, 3, 128]), op=ALU.mult),
                 reads=[a0, m], writes=[o1[eb]])
            k.op('dve', lambda e: e.tensor_tensor(out=ot[eb][:], in0=a1[:, :, 0:128],
                                                  in1=m[:, 2, :].unsqueeze(2).to_broadcast([128, 3, 128]), op=ALU.mult),
                 reads=[a1, m], writes=[ot[eb]])
            k.op('pool', lambda e: e.tensor_tensor(out=ot[eb][:], in0=ot[eb][:], in1=o1[eb][:], op=ALU.add),
                 reads=[ot[eb], o1[eb]], writes=[ot[eb]])
            k.op('pool', lambda e: e.tensor_tensor(out=sq[eb][:], in0=ot[eb][:], in1=ot[eb][:], op=ALU.mult),
                 reads=[ot[eb]], writes=[sq[eb]])
            k.op('dve', lambda e: e.reduce_sum(out=m[:, 3, :], in_=sq[eb][:], axis=AX.X), reads=[sq[eb]], writes=[m])
            k.op('dve', lambda e: e.tensor_scalar(m[:, 4, :], m[:, 3, :], 1.0 / 128, EPS, ALU.mult, ALU.add), reads=[m], writes=[m])

        def epi2(h, qb):
            eb = (h * 11 + qb) % 2
            m = sm[eb]
            k.op('act', lambda e: e.activation(out=m[:, 5, :], in_=m[:, 4, :], func=AF.Ln), reads=[m], writes=[m])
            k.op('act', lambda e: e.activation(out=m[:, 6, :], in_=m[:, 5, :], func=AF.Exp, scale=-0.5), reads=[m], writes=[m])
            k.op('dve', lambda e: e.tensor_tensor(out=ot[eb][:], in0=ot[eb][:],
                                                  in1=m[:, 6, :].unsqueeze(2).to_broadcast([128, 3, 128]), op=ALU.mult),
                 reads=[ot[eb], m], writes=[ot[eb]])
            k.op('pool', lambda e: e.tensor_tensor(out=ot[eb][:], in0=ot[eb][:],
                                                   in1=snw[:].unsqueeze(1).to_broadcast([128, 3, 128]), op=ALU.mult),
                 reads=[ot[eb], snw], writes=[ot[eb]])
            k.op('dve', lambda e: e.tensor_tensor(out=ob[eb][:], in0=ot[eb][:], in1=sg[eb][:], op=ALU.mult),
                 reads=[ot[eb], sg[eb]], writes=[ob[eb]])
            k.dma('pool', rows3(sc["MIX"], qb, 512 + h * 128), ob[eb][:], reads=[ob[eb]])

        cv = []
        if l + 1 < self.layers:
            wb = [k.tile([128, 16, 512], BF16, "wcv") for _ in range(2)]
            cv = self.wconv_steps(l + 1, wb)
        pend = []
        scores(0)
        for si in range(len(steps)):
            if cv and si % 60 == 30:
                cv.pop(0)()
            if si + 1 < len(steps):
                scores(si + 1)
            exps(si)
            pv(si)
            h, qb, j = steps[si]
            if qb == 0 and j == 0 and h + 1 < 4:
                load_head(h + 1)
            if j == 4:
                gate_pre(h, qb)
            if j == 8 and pend:
                epi2(*pend.pop(0))
            if j == NT - 1:
                epi1(h, qb)
                pend.append((h, qb))
        while pend:
            epi2(*pend.pop(0))
        while cv:
            cv.pop(0)()
        k.pop()

    def phase_S1(self, l):
        k = self.k
        sc = self.sc
        k.push()
        self.ones_col()
        identf = k.tile([128, 128], F32, "identf")
        k.dma('sp', identf[:], self.c_ident[:, :], writes=[identf])
        identb = k.tile([128, 128], BF16, "identb")
        k.dma('pool', identb[:], self.c_ident[:, :], writes=[identb])
        cw = k.tile([128, 12, 5], F32, "cw")
        for jj in range(5):
            k.dma('sp', cw[:, :, jj], self.conv_w[l, jj, :].rearrange("(c p) -> p c", p=128), writes=[cw], partial=True,
                  allow_slow_non_contiguous=True)
        cbv = k.tile([128, 12], F32, "cbv")
        k.dma('sp', cbv[:], self.conv_b[l, :].rearrange("(c p) -> p c", p=128), writes=[cbv],
              allow_slow_non_contiguous=True)
        xin = [k.tile([128, LP + 4], F32, "xin") for _ in range(2)]
        for b in range(2):
            k.op('dve', lambda e: e.memset(xin[b][:, 0:2], 0.0), writes=[xin[b]])
            k.op('dve', lambda e: e.memset(xin[b][:, LP + 2:LP + 4], 0.0), writes=[xin[b]], partial=True)
        acc = [k.tile([128, LP], F32, "cacc") for _ in range(2)]
        et = [k.tile([128, LP], F32, "cet") for _ in range(2)]
        yb = [k.tile([128, LP], BF16, "cyb") for _ in range(2)]
        pt = [k.psum([128, 4, 128], F32, "cpt") for _ in range(2)]
        ptb = [k.psum([128, 8, 128], BF16, "cptb") for _ in range(2)]
        xo = [k.tile([128, 4, 128], F32, "cxo") for _ in range(2)]
        bo = [k.tile([128, 8, 128], BF16, "cbo") for _ in range(2)]
        cn = dict(p=0, o=0)
        for j in range(12):
            b = j % 2
            eng = 'pool'
            xi, ac, ee = xin[b], acc[b], et[b]
            k.dma('sp', xi[:, 2:LP + 2], sc["XBCT"][j * 128:(j + 1) * 128, :], writes=[xi], partial=True)
            k.op('dve', lambda e: e.tensor_scalar(ac[:], xi[:, 0:LP], cw[:, j, 0:1], cbv[:, j:j + 1], ALU.mult, ALU.add),
                 reads=[xi, cw, cbv], writes=[ac])
            for jj in range(1, 5):
                k.op('dve', lambda e: e.scalar_tensor_tensor(out=ac[:], in0=xi[:, jj:jj + LP], scalar=cw[:, j, jj:jj + 1],
                                                           in1=ac[:], op0=ALU.mult, op1=ALU.add),
                     reads=[xi, cw, ac], writes=[ac])
            self.sigmoid_act(ee[:], ac[:], [ac], [ee])
            if j < 8:
                k.op(eng, lambda e: e.tensor_mul(ac[:], ac[:], ee[:]), reads=[ac, ee], writes=[ac])
                k.op(eng, lambda e: e.memset(ac[:, 0:PADR], 0.0), writes=[ac])
                for t0 in range(0, NT, 4):
                    n = min(4, NT - t0)
                    p = pt[cn['p'] % 2]
                    o = xo[cn['p'] % 2]
                    cn['p'] += 1
                    for i in range(n):
                        k.op('pe', lambda e: e.transpose(p[:, i, :], ac[:, (t0 + i) * 128:(t0 + i + 1) * 128], identf[:]),
                             reads=[ac, identf], writes=[p], partial=(i > 0))
                    k.op('act', lambda e: e.copy(out=o[:, 0:n, :], in_=p[:, 0:n, :]), reads=[p], writes=[o])
                    k.dma('pool', sc["XS"][t0 * 128:(t0 + n) * 128, j * 128:(j + 1) * 128].rearrange("(a p) c -> p a c", p=128),
                          o[:, 0:n, :], reads=[o])
            else:
                y = yb[b]
                k.op(eng, lambda e: e.tensor_mul(y[:], ac[:], ee[:]), reads=[ac, ee], writes=[y])
                k.op(eng, lambda e: e.memset(y[:, 0:PADR], 0.0), writes=[y])
                if j < 10:
                    g = j - 8
                    k.dma('pool', sc["BT"][g * 128:(g + 1) * 128, :], y[:], reads=[y])
                    for t0 in range(0, NT, 8):
                        n = min(8, NT - t0)
                        p = ptb[cn['o'] % 2]
                        o = bo[cn['o'] % 2]
                        cn['o'] += 1
                        for i in range(n):
                            k.op('pe', lambda e: e.transpose(p[:, i, :], y[:, (t0 + i) * 128:(t0 + i + 1) * 128], identb[:]),
                                 reads=[y, identb], writes=[p], partial=(i > 0))
                        k.op('act', lambda e: e.copy(out=o[:, 0:n, :], in_=p[:, 0:n, :]), reads=[p], writes=[o])
                        k.dma('pool', sc["BM"][t0 * 128:(t0 + n) * 128, g * 128:(g + 1) * 128].rearrange("(a p) c -> p a c", p=128),
                              o[:, 0:n, :], reads=[o])
                else:
                    g = j - 10
                    k.dma('pool', sc["CT"][g * 128:(g + 1) * 128, :], y[:], reads=[y])
        k.pop()

    def s1_gen(self, l):
        k = self.k
        sc = self.sc
        identf = k.tile([128, 128], F32, "identf")
        k.dma('sp', identf[:], self.c_ident[:, :], writes=[identf])
        identb = k.tile([128, 128], BF16, "identb")
        k.dma('pool', identb[:], self.c_ident[:, :], writes=[identb])
        cw = k.tile([128, 12, 5], F32, "cw")
        for jj in range(5):
            k.dma('sp', cw[:, :, jj], self.conv_w[l, jj, :].rearrange("(c p) -> p c", p=128), writes=[cw], partial=True,
                  allow_slow_non_contiguous=True)
        cbv = k.tile([128, 12], F32, "cbv")
        k.dma('sp', cbv[:], self.conv_b[l, :].rearrange("(c p) -> p c", p=128), writes=[cbv],
              allow_slow_non_contiguous=True)
        xin = [k.tile([128, LP + 4], F32, "xin") for _ in range(2)]
        for b in range(2):
            k.op('dve', lambda e: e.memset(xin[b][:, 0:2], 0.0), writes=[xin[b]])
            k.op('dve', lambda e: e.memset(xin[b][:, LP + 2:LP + 4], 0.0), writes=[xin[b]], partial=True)
        acc = [k.tile([128, LP], F32, "cacc") for _ in range(1)] * 2
        et = [k.tile([128, LP], F32, "cet") for _ in range(1)] * 2
        yb = [k.tile([128, LP], BF16, "cyb") for _ in range(1)] * 2
        pt = [k.psum([128, 4, 128], F32, "cpt") for _ in range(1)] * 2
        ptb = [k.psum([128, 8, 128], BF16, "cptb") for _ in range(1)] * 2
        xo = [k.tile([128, 4, 128], F32, "cxo") for _ in range(2)]
        bo = [k.tile([128, 8, 128], BF16, "cbo") for _ in range(2)]
        cn = dict(p=0, o=0)
        for j in range(12):
            b = j % 2
            eng = 'pool'
            xi, ac, ee = xin[b], acc[b], et[b]
            yield
            k.dma('sp', xi[:, 2:LP + 2], sc["XBCT"][j * 128:(j + 1) * 128, :], writes=[xi], partial=True)
            yield
            k.op('dve', lambda e: e.tensor_scalar(ac[:], xi[:, 0:LP], cw[:, j, 0:1], cbv[:, j:j + 1], ALU.mult, ALU.add),
                 reads=[xi, cw, cbv], writes=[ac])
            for jj in range(1, 5):
                yield
                k.op('dve', lambda e: e.scalar_tensor_tensor(out=ac[:], in0=xi[:, jj:jj + LP], scalar=cw[:, j, jj:jj + 1],
                                                           in1=ac[:], op0=ALU.mult, op1=ALU.add),
                     reads=[xi, cw, ac], writes=[ac])
            yield
            self.sigmoid_act(ee[:], ac[:], [ac], [ee])
            if j < 8:
                yield
                k.op(eng, lambda e: e.tensor_mul(ac[:], ac[:], ee[:]), reads=[ac, ee], writes=[ac])
                yield
                k.op(eng, lambda e: e.memset(ac[:, 0:PADR], 0.0), writes=[ac])
                for t0 in range(0, NT, 4):
                    n = min(4, NT - t0)
                    p = pt[cn['p'] % 2]
                    o = xo[cn['p'] % 2]
                    cn['p'] += 1
                    for i in range(n):
                        k.op('pe', lambda e: e.transpose(p[:, i, :], ac[:, (t0 + i) * 128:(t0 + i + 1) * 128], identf[:]),
                             reads=[ac, identf], writes=[p], partial=(i > 0))
                    yield
                    k.op('act', lambda e: e.copy(out=o[:, 0:n, :], in_=p[:, 0:n, :]), reads=[p], writes=[o])
                    yield
                    k.dma('pool', sc["XS"][t0 * 128:(t0 + n) * 128, j * 128:(j + 1) * 128].rearrange("(a p) c -> p a c", p=128),
                          o[:, 0:n, :], reads=[o])
            else:
                y = yb[b]
                yield
                k.op(eng, lambda e: e.tensor_mul(y[:], ac[:], ee[:]), reads=[ac, ee], writes=[y])
                yield
                k.op(eng, lambda e: e.memset(y[:, 0:PADR], 0.0), writes=[y])
                if j < 10:
                    g = j - 8
                    yield
                    k.dma('pool', sc["BT"][g * 128:(g + 1) * 128, :], y[:], reads=[y])
                    for t0 in range(0, NT, 8):
                        n = min(8, NT - t0)
                        p = ptb[cn['o'] % 2]
                        o = bo[cn['o'] % 2]
                        cn['o'] += 1
                        for i in range(n):
                            k.op('pe', lambda e: e.transpose(p[:, i, :], y[:, (t0 + i) * 128:(t0 + i + 1) * 128], identb[:]),
                                 reads=[y, identb], writes=[p], partial=(i > 0))
                        k.op('act', lambda e: e.copy(out=o[:, 0:n, :], in_=p[:, 0:n, :]), reads=[p], writes=[o])
                        k.dma('pool', sc["BM"][t0 * 128:(t0 + n) * 128, g * 128:(g + 1) * 128].rearrange("(a p) c -> p a c", p=128),
                              o[:, 0:n, :], reads=[o])
                else:
                    g = j - 10
                    yield
                    k.dma('pool', sc["CT"][g * 128:(g + 1) * 128, :], y[:], reads=[y])


    def load_tri(self, scale=None, name="tri"):
        k = self.k
        tri = k.tile([128, 4, 128], F32, name)
        k.dma('sp', tri[:], self.c_tri.rearrange("k s t -> s k t"), writes=[tri])
        if scale is not None:
            k.op('dve', lambda e: e.tensor_scalar_mul(tri[:], tri[:], scale), reads=[tri], writes=[tri])
        return tri

    def phase_S2(self, l):
        k = self.k
        sc = self.sc
        k.push()
        self.ones_col()
        identf = k.tile([128, 128], F32, "identf")
        k.dma('sp', identf[:], self.c_ident[:, :], writes=[identf])
        ones = k.tile([128, 128], F32, "ones")
        k.op('dve', lambda e: e.memset(ones[:], 1.0), writes=[ones])
        tri = self.load_tri()
        negm = k.tile([128, 2, 128], F32, "negm")
        k.op('dve', lambda e: e.tensor_scalar(negm[:], tri[:, 0:2, :], -NEG, NEG, ALU.mult, ALU.add), reads=[tri], writes=[negm])
        dtb = k.tile([128, 32], F32, "dtb")
        k.dma('sp', dtb[:], bc_rows(self.ssd_dt_bias[l:l + 1, :]), writes=[dtb])
        Av = k.tile([128, 32], F32, "Av")
        k.dma('sp', Av[:], bc_rows(self.ssd_A_log[l:l + 1, :]), writes=[Av])
        k.op('act', lambda e: e.activation(out=Av[:], in_=Av[:], func=AF.Exp), reads=[Av], writes=[Av])
        k.op('dve', lambda e: e.tensor_scalar_mul(Av[:], Av[:], -1.0), reads=[Av], writes=[Av])
        Dv = k.tile([128, 16], F32, "Dv")
        k.dma('sp', Dv[:], bc_rows(self.ssd_D[l:l + 1, :]), writes=[Dv])
        nw = k.tile([128, 1024], F32, "snw")
        k.dma('sp', nw[:], bc_rows(self.ssd_norm_w[l:l + 1, :]), writes=[nw])

        xs = [k.tile([128, 16, 64], F32, "xs") for _ in range(4)]
        bm = [k.tile([128, 256], BF16, "bm") for _ in range(4)]
        bt = [k.tile([128, 2, 128], BF16, "bt") for _ in range(4)]
        ct = [k.tile([128, 2, 128], BF16, "ct") for _ in range(4)]
        dtr = [k.tile([128, 32], F32, "dtr") for _ in range(4)]
        zt = [k.tile([128, 1024], F32, "zt") for _ in range(4)]
        yf = [k.tile([128, 1024], F32, "yf") for _ in range(4)]
        sm = [k.tile([128, 8, 16], F32, "ssm") for _ in range(3)]
        xd = [k.tile([128, 16, 64], BF16, "xd") for _ in range(3)]
        xdw = [k.tile([128, 16, 64], BF16, "xdw") for _ in range(3)]
        GT = [k.tile([128, 2, 128], F32, "GT") for _ in range(3)]
        Lt = [k.tile([128, 4, 128], F32, "Lt") for _ in range(2)]
        Mt = [k.tile([128, 4, 128], BF16, "Mt") for _ in range(2)]
        ysb = [k.tile([128, 16, 64], F32, "ysb") for _ in range(2)]
        ytmp = [k.tile([128, 16, 64], F32, "ytmp") for _ in range(2)]
        szt = [k.tile([128, 1024], F32, "szt") for _ in range(4)]
        pre = [k.tile([128, 16, 64], F32, "pre") for _ in range(4)]
        S = k.tile([128, 16, 64], F32, "S")
        Sb = k.tile([128, 16, 64], BF16, "Sb")
        St = k.tile([128, 16, 64], F32, "Stmp")
        ob = [k.tile([128, 1024], BF16, "sob") for _ in range(2)]
        m2 = [k.tile([128, 8], F32, "m2") for _ in range(2)]
        junk = k.tile([128, 512], F32, "sjunk")
        psY = [k.psum([128, 8, 64], F32, "psY") for _ in range(2)]
        psO = [k.psum([128, 8, 64], F32, "psO") for _ in range(2)]
        psS = k.psum([128, 8, 64], F32, "psS")
        psL = [k.psum([128, 4, 128], F32, "psL") for _ in range(2)]
        psM = k.psum([128, 512], F32, "psM")

        def loads(t, dd):
            b3 = t % 4
            r = slice(t * 128, (t + 1) * 128)
            k.dma('sp', xs[b3][:].rearrange("p h d -> p (h d)"), sc["XS"][r, :], writes=[xs[b3]])
            k.dma('sp', bm[b3][:], sc["BM"][r, :], writes=[bm[b3]])
            k.dma('sp', bt[b3][:], sc["BT"][:, r].rearrange("(g n) t -> n g t", g=2), writes=[bt[b3]])
            k.dma('sp', ct[b3][:], sc["CT"][:, r].rearrange("(g n) t -> n g t", g=2), writes=[ct[b3]])
            k.dma('sp', dtr[b3][:], sc["DT"][r, :], writes=[dtr[b3]])
            if dd == 1:
                k.dma('sp', zt[b3][:], sc["Z"][r, :], writes=[zt[b3]])
                k.dma('sp', yf[b3][:], sc["YF"][r, :], writes=[yf[b3]])

        lc = 0
        for dd in range(2):
            order = list(range(NT)) if dd == 0 else list(range(NT - 1, -1, -1))
            k.op('dve', lambda e: e.memset(S[:], 0.0), writes=[S])
            k.op('dve', lambda e: e.memset(Sb[:], 0.0), writes=[Sb])
            def stage1(t, dd=dd):
                b = t % 2
                b3 = t % 4
                bs = t % 3
                m = sm[bs]
                cs = slice(dd * 16, dd * 16 + 16)
                x_, bm_, bt_, ct_ = xs[b3], bm[b3], bt[b3], ct[b3]
                yield
                k.op('dve', lambda e: e.tensor_tensor(out=m[:, 0, :], in0=dtr[b3][:, cs], in1=dtb[:, cs], op=ALU.add),
                     reads=[dtr[b3], dtb], writes=[m])
                yield
                k.op('act', lambda e: e.activation(out=m[:, 0, :], in_=m[:, 0, :], func=AF.Exp), reads=[m], writes=[m])
                yield
                k.op('dve', lambda e: e.tensor_scalar_add(m[:, 0, :], m[:, 0, :], 1.0), reads=[m], writes=[m])
                yield
                k.op('act', lambda e: e.activation(out=m[:, 0, :], in_=m[:, 0, :], func=AF.Ln), reads=[m], writes=[m])
                yield
                k.op('dve', lambda e: e.tensor_tensor(out=m[:, 1, :], in0=m[:, 0, :], in1=Av[:, cs], op=ALU.mult),
                     reads=[m, Av], writes=[m])
                yield
                k.op('pe', lambda e: e.matmul(psM[:, 256:272], lhsT=tri[:, dd, :], rhs=m[:, 1, :], start=True, stop=True),
                     reads=[tri, m], writes=[psM], partial=True)
                yield
                k.op('pe', lambda e: e.matmul(psM[:, 272:288], lhsT=tri[:, 2 + dd, :], rhs=m[:, 1, :], start=True, stop=True),
                     reads=[tri, m], writes=[psM], partial=True)
                yield
                k.op('pe', lambda e: e.matmul(psM[:, 288:304], lhsT=ones[:], rhs=m[:, 1, :], start=True, stop=True),
                     reads=[ones, m], writes=[psM], partial=True)
                yield
                k.op('dve', lambda e: e.tensor_copy(out=m[:, 2, :], in_=psM[:, 256:272]), reads=[psM], writes=[m])
                yield
                k.op('dve', lambda e: e.tensor_scalar_mul(m[:, 3, :], m[:, 2, :], -1.0), reads=[m], writes=[m])
                yield
                k.op('act', lambda e: e.activation(out=m[:, 4:7, :], in_=psM[:, 256:304].rearrange("p (a h) -> p a h", a=3),
                                                   func=AF.Exp), reads=[psM], writes=[m])
                yield
                k.op('dve', lambda e: e.tensor_tensor(out=m[:, 7, :], in0=m[:, 0, :], in1=m[:, 5, :], op=ALU.mult),
                     reads=[m], writes=[m])
                yield
                k.op('pool', lambda e: e.tensor_tensor(out=xd[bs][:], in0=x_[:], in1=m[:, 0, :].unsqueeze(2).to_broadcast([128, 16, 64]),
                                                       op=ALU.mult), reads=[x_, m], writes=[xd[bs]])
                yield
                k.op('dve', lambda e: e.tensor_tensor(out=xdw[bs][:], in0=x_[:], in1=m[:, 7, :].unsqueeze(2).to_broadcast([128, 16, 64]),
                                                      op=ALU.mult), reads=[x_, m], writes=[xdw[bs]])
                yield
                for g in range(2):
                    k.op('pe', lambda e: e.matmul(psM[:, g * 128:(g + 1) * 128], lhsT=bt_[:, g, :], rhs=ct_[:, g, :],
                                                  start=True, stop=True), reads=[bt_, ct_], writes=[psM], partial=True)
                yield
                k.op('act', lambda e: e.copy(out=GT[bs][:].rearrange("p g t -> p (g t)"), in_=psM[:, 0:256]),
                     reads=[psM], writes=[GT[bs]])
                if dd == 1:
                    yield
                    self.sigmoid_act(szt[b3][:], zt[b3][:], [zt[b3]], [szt[b3]])
                    yield
                    k.op('pool', lambda e: e.tensor_mul(szt[b3][:], szt[b3][:], zt[b3][:]), reads=[szt[b3], zt[b3]], writes=[szt[b3]])
                    yield
                    k.op('pool', lambda e: e.tensor_tensor(out=pre[b3][:], in0=x_[:], in1=Dv[:, :].unsqueeze(2).to_broadcast([128, 16, 64]),
                                                           op=ALU.mult), reads=[x_, Dv], writes=[pre[b3]])
            def stage2(t, dd=dd):
                b = t % 2
                b3 = t % 4
                bs = t % 3
                m = sm[bs]
                cs = slice(dd * 16, dd * 16 + 16)
                x_, bm_, bt_, ct_ = xs[b3], bm[b3], bt[b3], ct[b3]
                yield
                for g in range(2):
                    k.op('pe', lambda e: e.matmul(psO[g][:].rearrange("p h d -> p (h d)"), lhsT=ct_[:, g, :],
                                                  rhs=Sb[:, g * 8:(g + 1) * 8, :].rearrange("p h d -> p (h d)"),
                                                  start=True, stop=True), reads=[ct_, Sb], writes=[psO[g]])
                yb_ = ysb[b]
                yt_ = ytmp[b]
                yield
                for g in range(2):
                    hs = slice(g * 8, (g + 1) * 8)
                    k.op('dve', lambda e: e.tensor_tensor(out=yt_[:, hs, :], in0=psO[g][:],
                                                          in1=m[:, 4, hs].unsqueeze(2).to_broadcast([128, 8, 64]), op=ALU.mult),
                         reads=[psO[g], m], writes=[yt_], partial=(g > 0))

                def Lmm(hg):
                    pl = psL[hg % 2]
                    for i in range(4):
                        h = hg * 4 + i
                        k.op('pe', lambda e: e.matmul(pl[:, i, :], lhsT=identf[:], rhs=negm[:, dd, :], start=(i == 0), stop=False),
                             reads=[identf, negm], writes=[pl], partial=(i > 0))
                        k.op('pe', lambda e: e.matmul(pl[:, i, :], lhsT=m[:, 1, h:h + 1].to_broadcast([128, 128]), rhs=tri[:, dd, :],
                                                      start=False, stop=True), reads=[m, tri], writes=[pl], partial=True)

                yield
                Lmm(0)
                yield
                for hg in range(4):
                    if hg + 1 < 4:
                        Lmm(hg + 1)
                    yield
                    g = hg // 2
                    pl = psL[hg % 2]
                    lt = Lt[hg % 2]
                    mt = Mt[hg % 2]
                    for i in range(4):
                        h = hg * 4 + i
                        k.op('act', lambda e: e.activation(out=lt[:, i, :], in_=pl[:, i, :], func=AF.Exp,
                                                           bias=m[:, 3, h:h + 1], scale=1.0),
                             reads=[pl, m], writes=[lt], partial=(i > 0))
                    k.op('dve', lambda e: e.tensor_tensor(out=mt[:], in0=lt[:],
                                                          in1=GT[bs][:, g, :].unsqueeze(1).to_broadcast([128, 4, 128]), op=ALU.mult),
                         reads=[lt, GT[bs]], writes=[mt])
                    for i in range(4):
                        h = hg * 4 + i
                        k.op('pe', lambda e: e.matmul(psY[g][:, h % 8, :], lhsT=mt[:, i, :], rhs=xd[bs][:, h, :],
                                                      start=(h % 8 == 0), stop=True), reads=[mt, xd[bs]], writes=[psY[g]],
                             partial=(h % 8 != 0))
                yield
                for g in range(2):
                    hs = slice(g * 8, (g + 1) * 8)
                    k.op('dve', lambda e: e.tensor_tensor(out=yb_[:, hs, :], in0=psY[g][:], in1=yt_[:, hs, :], op=ALU.add),
                         reads=[psY[g], yt_], writes=[yb_], partial=(g > 0))
                yield
                for g in range(2):
                    hs = slice(g * 8, (g + 1) * 8)
                    k.op('pe', lambda e: e.matmul(psS[:].rearrange("p h d -> p (h d)"), lhsT=bm_[:, g * 128:(g + 1) * 128],
                                                  rhs=xdw[bs][:, hs, :].rearrange("p h d -> p (h d)"), start=True, stop=True),
                         reads=[bm_, xdw[bs]], writes=[psS])
                    k.op('pool', lambda e: e.tensor_tensor(out=St[:, hs, :], in0=S[:, hs, :],
                                                           in1=m[:, 6, hs].unsqueeze(2).to_broadcast([128, 8, 64]), op=ALU.mult),
                         reads=[S, m], writes=[St], partial=(g > 0))
                    k.op('dve', lambda e: e.tensor_tensor(out=S[:, hs, :], in0=psS[:], in1=St[:, hs, :], op=ALU.add),
                         reads=[psS, St], writes=[S], partial=(g > 0))
                yield
                k.op('act', lambda e: e.copy(out=Sb[:], in_=S[:]), reads=[S], writes=[Sb])
                r = slice(t * 128, (t + 1) * 128)
                if dd == 0:
                    k.dma('pool', sc["YF"][r, :], yb_[:].rearrange("p h d -> p (h d)"), reads=[yb_])
            def stage3(t, dd=dd):
                b = t % 2
                b3 = t % 4
                bs = t % 3
                yb_ = ysb[b]
                r = slice(t * 128, (t + 1) * 128)
                yfl = yf[b3][:].rearrange("p (h d) -> p h d", h=16)
                k.op('dve', lambda e: e.tensor_tensor(out=yb_[:], in0=yb_[:], in1=yfl, op=ALU.add),
                     reads=[yb_, yf[b3]], writes=[yb_])
                yield
                k.op('pool', lambda e: e.tensor_tensor(out=yb_[:], in0=yb_[:], in1=pre[b3][:], op=ALU.add),
                     reads=[yb_, pre[b3]], writes=[yb_])
                yield
                yfl2 = yb_[:].rearrange("p h d -> p (h d)")
                k.op('dve', lambda e: e.tensor_tensor(out=yfl2, in0=yfl2, in1=szt[b3][:], op=ALU.mult),
                     reads=[yb_, szt[b3]], writes=[yb_])
                yield
                mm = m2[b]
                for g in range(2):
                    k.op('act', lambda e: e.activation(out=junk[:], in_=yfl2[:, g * 512:(g + 1) * 512], func=AF.Square,
                                                       accum_out=mm[:, g:g + 1]), reads=[yb_], writes=[junk, mm],
                         partial=(g > 0))
                yield
                self.rstd_col(mm[:, 4:6], mm[:, 0:2], 512, mm[:, 2:4], [mm], [mm])
                yield
                o_ = ob[b]
                for g in range(2):
                    k.op('dve',
                         lambda e: e.scalar_tensor_tensor(out=o_[:, g * 512:(g + 1) * 512], in0=yfl2[:, g * 512:(g + 1) * 512],
                                                          scalar=mm[:, 4 + g:5 + g], in1=nw[:, g * 512:(g + 1) * 512],
                                                          op0=ALU.mult, op1=ALU.mult),
                         reads=[yb_, mm, nw], writes=[o_], partial=(g > 0))
                    yield
                k.dma('pool', sc["MIX"][r, 1024:2048], o_[:], reads=[o_])
            loads(order[0], dd)
            loads(order[1], dd)
            loads(order[2], dd)
            for _ in stage1(order[0]):
                pass
            for _ in stage1(order[1]):
                pass
            for oi, t in enumerate(order):
                gens = [stage2(t)]
                if oi + 2 < NT:
                    gens.append(stage1(order[oi + 2]))
                if dd == 1 and oi >= 1:
                    gens.append(stage3(order[oi - 1]))
                self.merge(gens)
                if oi + 3 < NT:
                    loads(order[oi + 3], dd)
            if dd == 1:
                for _ in stage3(order[NT - 1]):
                    pass
            k.barrier()
        k.pop()

    def phase_B(self, l):
        k = self.k
        sc = self.sc
        k.push()
        self.ones_col()
        tri = self.load_tri()
        triS = self.load_tri(scale=-1.0 / 16.0, name="triS")
        wa = k.tile([17, 2, 256], F32, "wa")
        for z in range(2):
            k.dma('sp', wa[0:16, z, :], self.gla_wa2[l, z], writes=[wa], partial=True)
            k.dma('sp', wa[16:17, z, :], self.gla_ba[l, z:z + 1, :], writes=[wa], partial=True)
        gnw = k.tile([128, 128], F32, "gnw")
        k.dma('sp', gnw[:], bc_rows(self.gla_norm_w[l:l + 1, :]), writes=[gnw])
        gca = [k.tile([17, 128], F32, "gca") for _ in range(4)]
        for b in range(4):
            k.op('dve', lambda e: e.memset(gca[b][:], 1.0), writes=[gca[b]])
        qT = [k.tile([64, 4, 128], BF16, "gqT") for _ in range(4)]
        kT = [k.tile([64, 4, 128], BF16, "gkT") for _ in range(4)]
        km = [k.tile([128, 256], BF16, "gkm") for _ in range(4)]
        vt = [k.tile([128, 512], BF16, "gv") for _ in range(4)]
        gg = [k.tile([128, 512], F32, "ggt") for _ in range(4)]
        of = [k.tile([128, 512], F32, "gof") for _ in range(4)]
        sp_ = [k.tile([128, 256], F32, "gsp") for _ in range(3)]
        ebT = [k.tile([64, 4, 128], F32, "ebT") for _ in range(3)]
        enbT = [k.tile([64, 4, 128], F32, "enbT") for _ in range(3)]
        qin = [k.tile([64, 4, 128], BF16, "qin") for _ in range(3)]
        kin = [k.tile([64, 4, 128], BF16, "kin") for _ in range(3)]
        kout = [k.tile([128, 256], BF16, "kout") for _ in range(3)]
        ete = [k.tile([128, 256], F32, "ete") for _ in range(3)]
        att = [k.tile([128, 4, 128], BF16, "att") for _ in range(3)]
        osb = [k.tile([128, 4, 128], F32, "osb") for _ in range(2)]
        otm = [k.tile([128, 4, 128], F32, "gotm") for _ in range(2)]
        ob = [k.tile([128, 512], BF16, "gob") for _ in range(2)]
        m2 = [k.tile([128, 16], F32, "gm2") for _ in range(2)]
        junk = k.tile([128, 128], F32, "gjunk")
        sgt = [k.tile([128, 512], F32, "sgt") for _ in range(4)]
        S = k.tile([64, 4, 128], F32, "gS")
        Sb = k.tile([64, 4, 128], BF16, "gSb")
        psX = k.psum([128, 256], F32, "psX")
        psB = k.psum([64, 4, 128], F32, "psB")
        psE = k.psum([128, 256], F32, "psE")
        psA = k.psum([128, 4, 128], F32, "psA")
        psO = k.psum([128, 4, 128], F32, "gpsO")
        psS = k.psum([64, 4, 128], F32, "gpsS")

        s1g = self.s1_gen(l) if self.merge_s1 else None

        def loads(t, dd):
            b3 = t % 4
            r = slice(t * 128, (t + 1) * 128)
            k.dma('sp', gca[b3][0:16, :], sc["GCT"][dd * 16:(dd + 1) * 16, r], writes=[gca[b3]], partial=True)
            k.dma('sp', qT[b3][:], sc["GQT"][:, r].rearrange("(h k) t -> k h t", h=4), writes=[qT[b3]])
            k.dma('sp', kT[b3][:], sc["GKT"][:, r].rearrange("(h k) t -> k h t", h=4), writes=[kT[b3]])
            k.dma('sp', km[b3][:], sc["GK"][r, :], writes=[km[b3]])
            k.dma('sp', vt[b3][:], sc["GV"][r, :], writes=[vt[b3]])
            if dd == 1:
                k.dma('sp', gg[b3][:], sc["GG"][r, :], writes=[gg[b3]])
                k.dma('sp', of[b3][:], sc["OF"][r, :], writes=[of[b3]])

        for dd in range(2):
            order = list(range(NT)) if dd == 0 else list(range(NT - 1, -1, -1))
            last = 127 if dd == 0 else 0
            k.op('dve', lambda e: e.memset(S[:], 0.0), writes=[S])
            k.op('dve', lambda e: e.memset(Sb[:], 0.0), writes=[Sb])
            def stage1(t, dd=dd, last=last):
                b = t % 2
                b3 = t % 4
                bs = t % 3
                sp = sp_[bs]
                yield
                k.op('pe', lambda e: e.matmul(psX[:], lhsT=gca[b3][0:17, :], rhs=wa[0:17, dd, :], start=True, stop=True),
                     reads=[gca[b3], wa], writes=[psX])
                yield
                k.op('act', lambda e: e.activation(out=sp[:], in_=psX[:], func=AF.Exp, scale=-1.0), reads=[psX], writes=[sp])
                yield
                k.op('dve', lambda e: e.tensor_scalar_add(sp[:], sp[:], 1.0), reads=[sp], writes=[sp])
                yield
                k.op('act', lambda e: e.activation(out=sp[:], in_=sp[:], func=AF.Ln), reads=[sp], writes=[sp])
                yield
                for hd in range(4):
                    k.op('pe', lambda e: e.matmul(psB[:, hd, :], lhsT=sp[:, hd * 64:(hd + 1) * 64], rhs=triS[:, dd, :],
                                                  start=True, stop=True), reads=[sp, triS], writes=[psB], partial=(hd > 0))
                yield
                k.op('pe', lambda e: e.matmul(psE[:], lhsT=triS[:, 2 + dd, :], rhs=sp[:], start=True, stop=True),
                     reads=[sp, triS], writes=[psE])
                yield
                k.op('act', lambda e: e.activation(out=ebT[bs][:], in_=psB[:], func=AF.Exp), reads=[psB], writes=[ebT[bs]])
                yield
                k.op('act', lambda e: e.activation(out=enbT[bs][:], in_=psB[:], func=AF.Exp, scale=-1.0), reads=[psB], writes=[enbT[bs]])
                yield
                k.op('act', lambda e: e.activation(out=ete[bs][:], in_=psE[:], func=AF.Exp), reads=[psE], writes=[ete[bs]])
                yield
                k.op('dve', lambda e: e.tensor_tensor(out=qin[bs][:], in0=qT[b3][:], in1=ebT[bs][:], op=ALU.mult),
                     reads=[qT[b3], ebT[bs]], writes=[qin[bs]])
                yield
                k.op('pool', lambda e: e.tensor_tensor(out=kin[bs][:], in0=kT[b3][:], in1=enbT[bs][:], op=ALU.mult),
                     reads=[kT[b3], enbT[bs]], writes=[kin[bs]])
                yield
                k.op('pool', lambda e: e.tensor_tensor(out=kout[bs][:], in0=km[b3][:], in1=ete[bs][:], op=ALU.mult),
                     reads=[km[b3], ete[bs]], writes=[kout[bs]])
                yield
                for hd in range(4):
                    k.op('pe', lambda e: e.matmul(psA[:, hd, :], lhsT=kin[bs][:, hd, :], rhs=qin[bs][:, hd, :], start=True, stop=True),
                         reads=[kin[bs], qin[bs]], writes=[psA], partial=(hd > 0))
                yield
                k.op('dve', lambda e: e.tensor_tensor(out=att[bs][:], in0=psA[:],
                                                      in1=tri[:, dd, :].unsqueeze(1).to_broadcast([128, 4, 128]), op=ALU.mult),
                     reads=[psA, tri], writes=[att[bs]])
                if dd == 1:
                    yield
                    self.sigmoid_act(sgt[b3][:], gg[b3][:], [gg[b3]], [sgt[b3]])
                    yield
                    k.op('pool', lambda e: e.tensor_mul(sgt[b3][:], sgt[b3][:], gg[b3][:]), reads=[sgt[b3], gg[b3]], writes=[sgt[b3]])
            def stage2(t, dd=dd, last=last):
                b = t % 2
                b3 = t % 4
                bs = t % 3
                yield
                for hd in range(4):
                    k.op('pe', lambda e: e.matmul(psO[:, hd, :], lhsT=att[bs][:, hd, :], rhs=vt[b3][:, hd * 128:(hd + 1) * 128],
                                                  start=(hd == 0), stop=False), reads=[att[bs], vt[b3]], writes=[psO], partial=(hd > 0))
                    k.op('pe', lambda e: e.matmul(psO[:, hd, :], lhsT=qin[bs][:, hd, :], rhs=Sb[:, hd, :], start=False, stop=True),
                         reads=[qin[bs], Sb], writes=[psO], partial=True)
                yield
                for hd in range(4):
                    k.op('pe', lambda e: e.matmul(psS[:, hd, :], lhsT=kout[bs][:, hd * 64:(hd + 1) * 64],
                                                  rhs=vt[b3][:, hd * 128:(hd + 1) * 128], start=True, stop=True),
                         reads=[kout[bs], vt[b3]], writes=[psS], partial=(hd > 0))
                yield
                for hd in range(4):
                    k.op('dve', lambda e: e.scalar_tensor_tensor(out=S[:, hd, :], in0=S[:, hd, :], scalar=ebT[bs][:, hd, last:last + 1],
                                                                  in1=psS[:, hd, :], op0=ALU.mult, op1=ALU.add),
                         reads=[S, ebT[bs], psS], writes=[S], partial=(hd > 0))
                r = slice(t * 128, (t + 1) * 128)
                o_ = osb[b]
                if dd == 0:
                    k.op('act', lambda e: e.copy(out=o_[:], in_=psO[:]), reads=[psO], writes=[o_])
                    k.op('act', lambda e: e.copy(out=Sb[:], in_=S[:]), reads=[S], writes=[Sb])
                    k.dma('pool', sc["OF"][r, :], o_[:].rearrange("p h d -> p (h d)"), reads=[o_])
                else:
                    k.op('dve', lambda e: e.tensor_tensor(out=o_[:], in0=psO[:], in1=of[b3][:].rearrange("p (h d) -> p h d", h=4),
                                                          op=ALU.add), reads=[psO, of[b3]], writes=[o_])
                    k.op('act', lambda e: e.copy(out=Sb[:], in_=S[:]), reads=[S], writes=[Sb])
            def stage3(t, dd=dd, last=last):
                b = t % 2
                b3 = t % 4
                bs = t % 3
                o_ = osb[b]
                r = slice(t * 128, (t + 1) * 128)
                mm = m2[b]
                for hd in range(4):
                    k.op('act', lambda e: e.activation(out=junk[:], in_=o_[:, hd, :], func=AF.Square, accum_out=mm[:, hd:hd + 1]),
                         reads=[o_], writes=[junk, mm], partial=(hd > 0))
                yield
                self.rstd_col(mm[:, 8:12], mm[:, 0:4], 128, mm[:, 4:8], [mm], [mm])
                yield
                k.op('dve', lambda e: e.tensor_tensor(out=o_[:], in0=o_[:], in1=mm[:, 8:12].unsqueeze(2).to_broadcast([128, 4, 128]),
                                                      op=ALU.mult), reads=[o_, mm], writes=[o_])
                yield
                k.op('pool', lambda e: e.tensor_tensor(out=o_[:], in0=o_[:], in1=gnw[:].unsqueeze(1).to_broadcast([128, 4, 128]),
                                                       op=ALU.mult), reads=[o_, gnw], writes=[o_])
                yield
                k.op('dve', lambda e: e.tensor_tensor(out=ob[b][:], in0=o_[:].rearrange("p h d -> p (h d)"), in1=sgt[b3][:], op=ALU.mult),
                     reads=[o_, sgt[b3]], writes=[ob[b]])
                yield
                k.dma('pool', sc["MIX"][r, 0:512], ob[b][:], reads=[ob[b]])
            loads(order[0], dd)
            loads(order[1], dd)
            loads(order[2], dd)
            for _ in stage1(order[0]):
                pass
            for _ in stage1(order[1]):
                pass
            for oi, t in enumerate(order):
                gens = [stage2(t)]
                if oi + 2 < NT:
                    gens.append(stage1(order[oi + 2]))
                if dd == 1 and oi >= 1:
                    gens.append(stage3(order[oi - 1]))
                self.merge(gens)
                if s1g is not None:
                    for _ in range(6):
                        next(s1g, None)
                if oi + 3 < NT:
                    loads(order[oi + 3], dd)
            if dd == 1:
                for _ in stage3(order[NT - 1]):
                    pass
            k.barrier()
        if s1g is not None:
            for _ in s1g:
                pass
        k.pop()

    def phase_D(self, l):
        k = self.k
        sc = self.sc
        H = sc["H"]
        k.push()
        self.load_consts()
        identb = self.identb
        TB = 8
        mT = k.tile([128, 16, TB * 128], BF16, "mT")
        mb = [k.tile([128, D], BF16, "mx") for _ in range(2)]
        hb = [k.tile([128, D], F32, "hD") for _ in range(TB)]
        pst = [k.psum([128, 8, 128], BF16, "pstD") for _ in range(2)]
        psm = [k.psum([128, 512], F32, "psmD") for _ in range(4)]
        wt = [k.tile([128, 16, 512], BF16, "wD") for _ in range(2)]
        W = self.sc["WOB"][l].rearrange("(c p) n -> p c n", p=128)
        blocks = [(0, 1)] + [(1 + 8 * i, 8) for i in range(4)]
        cn = dict(m=0, p=0, q=0)
        for (t0, nt) in blocks:
            for ti in range(nt):
                t = t0 + ti
                mx = mb[cn['m'] % 2]
                cn['m'] += 1
                k.dma('sp', mx[:], sc["MIX"][t * 128:(t + 1) * 128, :], writes=[mx])
                k.dma('sp', hb[ti][:], H[t * 128:(t + 1) * 128, :], writes=[hb[ti]])
                for half in range(2):
                    p = pst[cn['p'] % 2]
                    cn['p'] += 1
                    for c in range(8):
                        cc = half * 8 + c
                        k.op('pe', lambda e: e.transpose(p[:, c, :], mx[:, cc * 128:(cc + 1) * 128], identb[:]),
                             reads=[mx, identb], writes=[p], partial=(c > 0))
                    if half == 0:
                        k.op('act', lambda e: e.copy(out=mT[:, 0:8, ti * 128:(ti + 1) * 128], in_=p[:]),
                             reads=[p], writes=[mT], partial=True)
                    else:
                        k.op('dve', lambda e: e.tensor_copy(out=mT[:, 8:16, ti * 128:(ti + 1) * 128], in_=p[:]),
                             reads=[p], writes=[mT], partial=True)
            k.dma('pool', wt[0][:], W[:, :, 0:512], writes=[wt[0]])
            for j in range(4):
                if j + 1 < 4:
                    k.dma('pool', wt[(j + 1) % 2][:], W[:, :, (j + 1) * 512:(j + 2) * 512], writes=[wt[(j + 1) % 2]])
                w = wt[j % 2]
                for ti in range(nt):
                    p = psm[cn['q'] % 4]
                    cn['q'] += 1
                    for c in range(16):
                        k.op('pe', lambda e: e.matmul(p[:], lhsT=mT[:, c, ti * 128:(ti + 1) * 128], rhs=w[:, c, :],
                                                      start=(c == 0), stop=(c == 15)), reads=[w, mT], writes=[p], partial=(c > 0))
                    k.op('dve', lambda e: e.tensor_tensor(out=hb[ti][:, j * 512:(j + 1) * 512], in0=p[:],
                                                          in1=hb[ti][:, j * 512:(j + 1) * 512], op=ALU.add),
                         reads=[p, hb[ti]], writes=[hb[ti]], partial=True)
            for ti in range(nt):
                t = t0 + ti
                k.dma('pool', H[t * 128:(t + 1) * 128, :], hb[ti][:], reads=[hb[ti]])
        k.pop()

    def phase_F(self):
        k = self.k
        H = self.sc["H"]
        k.push()
        nwb = k.tile([128, D], F32, "fnw")
        k.dma('sp', nwb[:], bc_rows(self.final_norm_w[0:1, :]), writes=[nwb])
        hb = [k.tile([128, D], F32, "hF") for _ in range(3)]
        ob = [k.tile([128, D], F32, "oF") for _ in range(3)]
        junk = k.tile([128, D], BF16, "junkF")
        ss = [k.tile([128, 4], F32, "ssF") for _ in range(3)]
        for t in range(1, NT):
            i = t % 3
            h, o, s2 = hb[i], ob[i], ss[i]
            k.dma('sp', h[:], H[t * 128:(t + 1) * 128, :], writes=[h])
            k.op('act', lambda e: e.activation(out=junk[:], in_=h[:], func=AF.Square, accum_out=s2[:, 0:1]),
                 reads=[h], writes=[junk, s2])
            self.rstd_col(s2[:, 2:3], s2[:, 0:1], D, s2[:, 1:2], [s2], [s2])
            k.op('dve', lambda e: e.scalar_tensor_tensor(out=o[:], in0=h[:], scalar=s2[:, 2:3], in1=nwb[:],
                                                          op0=ALU.mult, op1=ALU.mult), reads=[h, s2, nwb], writes=[o])
            k.dma('pool', self.out[(t - 1) * 128:t * 128, :], o[:], reads=[o])
        k.pop()

    def build(self):
        k = self.k
        if self.want('init'):
            self.phase_init()
        if self.want('prep'):
            self.phase_prep()
        for l in range(self.layers):
            if self.want('A'):
                self.phase_A(l)
            if self.want('B'):
                self.phase_B(l)
            if self.want('C'):
                self.phase_C(l)
            if self.want('S1') and not self.merge_s1:
                self.phase_S1(l)
            if self.want('S2'):
                self.phase_S2(l)
            if self.want('D'):
                self.phase_D(l)
        if self.want('F'):
            self.phase_F()
        k.push()
        k.pop()
        return k.nc


def _t5_bucket(rel):
    nb = 16
    max_exact = 8
    ret = nb if rel > 0 else 0
    n = abs(rel)
    if n < max_exact:
        return ret + n
    nf = np.float32(max(n, 1))
    large = max_exact + int(np.float32(np.log(nf / np.float32(max_exact))) / np.float32(math.log(128 / max_exact))
                            * np.float32(nb - max_exact))
    return ret + min(large, nb - 1)


def host_consts():
    ident = np.eye(128, dtype=np.float32)
    s = np.arange(128)[:, None]
    t = np.arange(128)[None, :]
    tri = np.stack([(s <= t), (s >= t), (s > t), (s < t)]).astype(np.float32)
    onehot = np.zeros((32, 512), np.float32)
    for m_ in range(511):
        onehot[_t5_bucket(255 - m_), m_] = 1.0
    return dict(c_ident=ident, c_tri=tri, c_onehot=onehot)


def make_in_maps(inputs):
    consts = host_consts()
    maps = []
    shared = {}
    for n in ("meta_tokens", "rel_bias", "norm_w", "w_in", "w_out", "gla_wa2", "gla_norm_w",
              "diff_norm_w", "conv_w", "conv_b", "ssd_D", "ssd_norm_w"):
        shared[n] = np.ascontiguousarray(np.asarray(inputs[n], dtype=np.float32))
    shared["final_norm_w"] = np.asarray(inputs["final_norm_w"], np.float32).reshape(1, D)
    shared["gla_ba"] = np.asarray(inputs["gla_ba"], np.float32).reshape(DEPTH, 2, 256)
    shared["diff_lambda"] = np.asarray(inputs["diff_lambda"], np.float32).reshape(DEPTH, 256)
    shared["ssd_A_log"] = np.asarray(inputs["ssd_A_log"], np.float32).reshape(DEPTH, 32)
    shared["ssd_dt_bias"] = np.asarray(inputs["ssd_dt_bias"], np.float32).reshape(DEPTH, 32)
    shared.update(consts)
    x = np.asarray(inputs["x"], np.float32)
    for b in range(8):
        m = dict(shared)
        m["x"] = np.ascontiguousarray(x[b])
        maps.append(m)
    return maps


def kernel(**inputs):
    m = Model()
    nc = m.build()
    maps = make_in_maps(inputs)
    res = run_bass_kernel_spmd(nc, maps, core_ids=list(range(8)))
    out = np.stack([np.asarray(res.results[b]["out"]) for b in range(8)], axis=0)
    return out.astype(np.float32)
```
